# Optimizing a Trainium2 kernel written in Bass

```python
import jax, jax.numpy as jnp
from jax import lax
import numpy as np

D_MODEL = 2048
BATCH = 2
SEQ = 16384
DEPTH = 2

HEAD_DIM = 128
N_MIX_HEADS = D_MODEL // HEAD_DIM
N_SB_HEADS = N_MIX_HEADS // 4
N_GLA_HEADS = N_MIX_HEADS - N_SB_HEADS
GLA_DK = HEAD_DIM // 2
GLA_GATE_RANK = 16
GLA_GATE_TAU = 16.0
GLA_CHUNK = 16
SB_BLOCK = 128
CONV_WIDTH = 3
N_MEM = 256
N_XA_HEADS = 4
XA_WIDTH = N_XA_HEADS * HEAD_DIM
D_FF = 2 * D_MODEL
N_EVEN = (DEPTH + 1) // 2
N_ODD = DEPTH // 2
EPS = 1e-6

SB_W = N_SB_HEADS * HEAD_DIM
GLA_K_W = N_GLA_HEADS * GLA_DK
GLA_V_W = N_GLA_HEADS * HEAD_DIM
MIX_WIDTH = SB_W + GLA_V_W
AB_SPLIT_SIZES = (SB_W, SB_W, SB_W, GLA_K_W, GLA_K_W, GLA_V_W, GLA_V_W, GLA_GATE_RANK)
AB_IN_WIDTH = sum(AB_SPLIT_SIZES)
AB_SPLIT_POINTS = [int(v) for v in np.cumsum(AB_SPLIT_SIZES)[:-1]]

kernel_name = "hybrid_stickbreak_gla_shortconv_macaron"


def rmsnorm(x, g):
    xf = x.astype(jnp.float32)
    y = xf * lax.rsqrt(jnp.mean(xf * xf, axis=-1, keepdims=True) + EPS)
    return (y * g.astype(jnp.float32)).astype(x.dtype)


def swiglu_ffn(h, w_gu, w_down):
    gate, up = jnp.split(h @ w_gu, 2, axis=-1)
    return (jax.nn.silu(gate) * up) @ w_down


def to_heads(t, n_heads):
    b, s, w = t.shape
    return t.reshape(b, s, n_heads, w // n_heads).transpose(0, 2, 1, 3)


def from_heads(t):
    b, n, s, d = t.shape
    return t.transpose(0, 2, 1, 3).reshape(b, s, n * d)


def stick_breaking_attention(q, k, v):
    b, h, s, d = q.shape
    nb = s // SB_BLOCK
    qf = q.astype(jnp.float32) * (d ** -0.5)
    kf = k.astype(jnp.float32)
    vf = v.astype(jnp.float32)
    incl_lower = jnp.tril(jnp.ones((SB_BLOCK, SB_BLOCK), jnp.float32))
    outs = []
    for i in range(nb):
        nk = i + 1
        length = nk * SB_BLOCK
        qb = qf[:, :, i * SB_BLOCK:(i + 1) * SB_BLOCK]
        z = jnp.einsum('bhqd,bhkd->bhqk', qb, kf[:, :, :length])
        q_pos = i * SB_BLOCK + jnp.arange(SB_BLOCK)
        causal = jnp.arange(length)[None, :] < q_pos[:, None]
        sp = jnp.where(causal, jax.nn.softplus(z), 0.0).reshape(b, h, SB_BLOCK, nk, SB_BLOCK)
        within = jnp.einsum('bhqnj,js->bhqns', sp, incl_lower)
        later_mat = jnp.tril(jnp.ones((nk, nk), jnp.float32), -1)
        later = jnp.einsum('bhqm,mn->bhqn', sp.sum(axis=-1), later_mat)
        log_w = z.reshape(b, h, SB_BLOCK, nk, SB_BLOCK) - within - later[..., None]
        w = jnp.where(causal.reshape(SB_BLOCK, nk, SB_BLOCK), jnp.exp(log_w), 0.0)
        outs.append(jnp.einsum('bhqk,bhkd->bhqd', w.reshape(b, h, SB_BLOCK, length),
                               vf[:, :, :length]))
    return jnp.concatenate(outs, axis=2)


def gla_chunked(q, k, v, log_a):
    b, h, s, dk = q.shape
    dv = v.shape[-1]
    n = s // GLA_CHUNK

    def to_chunks(t):
        return t.astype(jnp.float32).reshape(b, h, n, GLA_CHUNK, t.shape[-1]).transpose(2, 0, 1, 3, 4)

    qc = to_chunks(q.astype(jnp.float32) * (dk ** -0.5))
    kc, vc, gc = to_chunks(k), to_chunks(v), to_chunks(log_a)
    incl = jnp.tril(jnp.ones((GLA_CHUNK, GLA_CHUNK), dtype=bool))

    def step(state, inp):
        qi, ki, vi, gi = inp
        bcum = jnp.cumsum(gi, axis=2)
        inter = jnp.einsum('bhcd,bhde->bhce', qi * jnp.exp(bcum), state)
        diff = bcum[:, :, :, None, :] - bcum[:, :, None, :, :]
        decay = jnp.exp(jnp.where(incl[:, :, None], diff, -jnp.inf))
        scores = jnp.einsum('bhtd,bhtsd,bhsd->bhts', qi, decay, ki)
        intra = jnp.einsum('bhts,bhse->bhte', scores, vi)
        b_last = bcum[:, :, -1:, :]
        new_state = (jnp.exp(b_last[:, :, 0, :])[..., None] * state
                     + jnp.einsum('bhcd,bhce->bhde', ki * jnp.exp(b_last - bcum), vi))
        return new_state, inter + intra

    state0 = jnp.zeros((b, h, dk, dv), jnp.float32)
    _, out = lax.scan(step, state0, (qc, kc, vc, gc))
    return out.transpose(1, 2, 0, 3, 4).reshape(b, h, s, dv)


def sb_gla_mixer(h, w_in, sb_qg, sb_kg, gla_w_gate, gla_b_gate, gla_og, w_out):
    proj = h @ w_in
    q_sb, k_sb, v_sb, q_g, k_g, v_g, r_g, g_lr = jnp.split(proj, AB_SPLIT_POINTS, axis=-1)
    o_sb = stick_breaking_attention(rmsnorm(to_heads(q_sb, N_SB_HEADS), sb_qg),
                                    rmsnorm(to_heads(k_sb, N_SB_HEADS), sb_kg),
                                    to_heads(v_sb, N_SB_HEADS))
    gate_logit = (g_lr @ gla_w_gate + gla_b_gate).astype(jnp.float32)
    log_a = jax.nn.log_sigmoid(gate_logit) / GLA_GATE_TAU
    o_g = gla_chunked(to_heads(q_g, N_GLA_HEADS), to_heads(k_g, N_GLA_HEADS),
                      to_heads(v_g, N_GLA_HEADS), to_heads(log_a, N_GLA_HEADS))
    o_g = from_heads(rmsnorm(o_g, gla_og)).astype(h.dtype) * jax.nn.silu(r_g)
    mixed = jnp.concatenate([from_heads(o_sb).astype(h.dtype), o_g], axis=-1)
    return mixed @ w_out


def short_conv_mixer(h, w_in, conv_w, w_out):
    bg, cg, u = jnp.split(h @ w_in, 3, axis=-1)
    y = lax.conv_general_dilated(cg * u, conv_w[:, None, :], window_strides=(1,),
                                 padding=[(CONV_WIDTH - 1, 0)],
                                 dimension_numbers=('NWC', 'WIO', 'NWC'),
                                 feature_group_count=D_MODEL)
    return (bg * y) @ w_out


def memory_cross_attention(h, m, w_q, w_kv, q_g, k_g, w_o):
    q = rmsnorm(to_heads(h @ w_q, N_XA_HEADS), q_g)
    k_m, v_m = jnp.split(m @ w_kv, 2, axis=-1)
    k = rmsnorm(to_heads(k_m, N_XA_HEADS), k_g)
    v = to_heads(v_m, N_XA_HEADS)
    scores = jnp.einsum('bhsd,bhmd->bhsm', q.astype(jnp.float32), k.astype(jnp.float32)) * (HEAD_DIM ** -0.5)
    p = jax.nn.softmax(scores, axis=-1)
    o = jnp.einsum('bhsm,bhmd->bhsd', p, v.astype(jnp.float32))
    return from_heads(o).astype(h.dtype) @ w_o


def setup_inputs(seed: int = 0) -> dict:
    key = jax.random.key(seed)
    ks = jax.random.split(key, 26)
    f32 = jnp.float32
    L, NE, NO = DEPTH, N_EVEN, N_ODD

    def dense(k, shape, fan_in):
        return jax.random.normal(k, shape, f32) * (fan_in ** -0.5)

    def gain(k, shape):
        return 1.0 + 0.02 * jax.random.normal(k, shape, f32)

    return {
        "x": jax.random.normal(ks[0], (BATCH, SEQ, D_MODEL), f32),
        "mem": jax.random.normal(ks[1], (BATCH, N_MEM, D_MODEL), f32),
        "ffn1_norm": gain(ks[2], (L, D_MODEL)),
        "ffn1_w_gu": dense(ks[3], (L, D_MODEL, 2 * D_FF), D_MODEL),
        "ffn1_w_down": dense(ks[4], (L, D_FF, D_MODEL), D_FF),
        "mix_norm": gain(ks[5], (L, D_MODEL)),
        "ab_w_in": dense(ks[6], (NE, D_MODEL, AB_IN_WIDTH), D_MODEL),
        "sb_q_norm": gain(ks[7], (NE, HEAD_DIM)),
        "sb_k_norm": gain(ks[8], (NE, HEAD_DIM)),
        "gla_w_gate": dense(ks[9], (NE, GLA_GATE_RANK, GLA_K_W), GLA_GATE_RANK),
        "gla_b_gate": 0.1 * jax.random.normal(ks[10], (NE, GLA_K_W), f32),
        "gla_o_norm": gain(ks[11], (NE, HEAD_DIM)),
        "ab_w_out": dense(ks[12], (NE, MIX_WIDTH, D_MODEL), MIX_WIDTH),
        "conv_w_in": dense(ks[13], (NO, D_MODEL, 3 * D_MODEL), D_MODEL),
        "conv_w": dense(ks[14], (NO, CONV_WIDTH, D_MODEL), CONV_WIDTH),
        "conv_w_out": dense(ks[15], (NO, D_MODEL, D_MODEL), D_MODEL),
        "xa_norm": gain(ks[16], (L, D_MODEL)),
        "mem_norm": gain(ks[17], (L, D_MODEL)),
        "xa_w_q": dense(ks[18], (L, D_MODEL, XA_WIDTH), D_MODEL),
        "xa_w_kv": dense(ks[19], (L, D_MODEL, 2 * XA_WIDTH), D_MODEL),
        "xa_q_norm": gain(ks[20], (L, HEAD_DIM)),
        "xa_k_norm": gain(ks[21], (L, HEAD_DIM)),
        "xa_w_o": dense(ks[22], (L, XA_WIDTH, D_MODEL), XA_WIDTH),
        "ffn2_norm": gain(ks[23], (L, D_MODEL)),
        "ffn2_w_gu": dense(ks[24], (L, D_MODEL, 2 * D_FF), D_MODEL),
        "ffn2_w_down": dense(ks[25], (L, D_FF, D_MODEL), D_FF),
    }


def reference(x, mem, ffn1_norm, ffn1_w_gu, ffn1_w_down, mix_norm, ab_w_in, sb_q_norm,
              sb_k_norm, gla_w_gate, gla_b_gate, gla_o_norm, ab_w_out, conv_w_in, conv_w,
              conv_w_out, xa_norm, mem_norm, xa_w_q, xa_w_kv, xa_q_norm, xa_k_norm, xa_w_o,
              ffn2_norm, ffn2_w_gu, ffn2_w_down):
    for layer in range(DEPTH):
        x = x + 0.5 * swiglu_ffn(rmsnorm(x, ffn1_norm[layer]), ffn1_w_gu[layer], ffn1_w_down[layer])
        h = rmsnorm(x, mix_norm[layer])
        i = layer // 2
        if layer % 2 == 0:
            x = x + sb_gla_mixer(h, ab_w_in[i], sb_q_norm[i], sb_k_norm[i], gla_w_gate[i],
                                 gla_b_gate[i], gla_o_norm[i], ab_w_out[i])
        else:
            x = x + short_conv_mixer(h, conv_w_in[i], conv_w[i], conv_w_out[i])
        x = x + memory_cross_attention(rmsnorm(x, xa_norm[layer]), rmsnorm(mem, mem_norm[layer]),
                                       xa_w_q[layer], xa_w_kv[layer], xa_q_norm[layer],
                                       xa_k_norm[layer], xa_w_o[layer])
        x = x + 0.5 * swiglu_ffn(rmsnorm(x, ffn2_norm[layer]), ffn2_w_gu[layer], ffn2_w_down[layer])
    return x
```

```python
import contextlib
import numpy as np
import concourse.bass as bass
import concourse.mybir as mybir
from concourse.bass_utils import run_bass_kernel_spmd

F32 = mybir.dt.float32
BF16 = mybir.dt.bfloat16
ALU = mybir.AluOpType
AF = mybir.ActivationFunctionType

D = 2048
KC = 16
NCORES = 8
EPS = 1e-6
COMPUTE = ("pe", "act", "dve", "pool")
N_DMA_SEMS = 48
N_HW_SEMS = 24
N_CC_SEMS = 12


class _Op:
    __slots__ = ("eng", "fn", "is_dma", "pos", "waits", "signal", "sigval",
                 "dsem", "dval", "clock", "inc")


class Prog:
    def __init__(self, nc):
        self.nc = nc
        self.ops = {e: [] for e in ("pe", "act", "dve", "pool", "sp")}
        self.seen = {e: {} for e in self.ops}
        self.last_w = {}
        self.readers = {}
        self.dma_i = {"hw": 0, "sw": 0}
        self.dma_last = [None] * (N_DMA_SEMS + N_CC_SEMS)
        self.dma_val = [0] * (N_DMA_SEMS + N_CC_SEMS)
        self.n_cc = 0
        self.pending = {e: [] for e in self.ops}
        self.out_dmas = []
        self.npos = {e: 0 for e in COMPUTE}
        self.uid = 0

    def _need(self, eng, dep, need):
        if dep.is_dma:
            key = ("d", dep.dsem)
            val = dep.dval
        else:
            key = dep.eng
            val = dep.pos
        if self.seen[eng].get(key, 0) >= val:
            return
        if need.get(key, (0, None))[0] < val:
            need[key] = (val, dep)

    def barrier(self):
        lasts = []
        for e in COMPUTE:
            for o in reversed(self.ops[e]):
                if not o.is_dma:
                    lasts.append(o)
                    break
        dmas = [o for o in self.dma_last if o is not None]
        for e in self.pending:
            self.pending[e] = lasts + dmas

    def op(self, eng, fn, reads=(), writes=(), dma=False, is_out=False, cc=False):
        o = _Op()
        o.eng = eng
        o.fn = fn
        o.is_dma = dma
        o.signal = False
        o.sigval = None
        need = {}
        if self.pending[eng]:
            for d in self.pending[eng]:
                if d.is_dma or d.eng != eng:
                    self._need(eng, d, need)
            self.pending[eng] = []

        def same(d):
            return (not dma) and (not d.is_dma) and d.eng == eng
        for r in reads:
            w = self.last_w.get(r)
            if w is not None and not (same(w) and eng == "pe"):
                self._need(eng, w, need)
        for r in writes:
            w = self.last_w.get(r)
            if w is not None and not same(w):
                self._need(eng, w, need)
            for rd in self.readers.get(r, ()):
                if not same(rd):
                    self._need(eng, rd, need)
        o.inc = 16
        if dma:
            if cc:
                i = N_DMA_SEMS + self.n_cc
                self.n_cc += 1
                assert self.n_cc <= N_CC_SEMS
                o.inc = 1
            elif eng == "pool":
                i = N_HW_SEMS + self.dma_i["sw"] % (N_DMA_SEMS - N_HW_SEMS)
                self.dma_i["sw"] += 1
            else:
                i = self.dma_i["hw"] % N_HW_SEMS
                self.dma_i["hw"] += 1
            prev = self.dma_last[i]
            if prev is not None:
                self._need(eng, prev, need)
            o.dsem = i
            self.dma_val[i] += o.inc
            o.dval = self.dma_val[i]
            self.dma_last[i] = o
            o.pos = None
        else:
            self.npos[eng] += 1
            o.pos = self.npos[eng]
        o.waits = []
        seen = self.seen[eng]
        for key, (val, dep) in need.items():
            if seen.get(key, 0) >= val:
                continue
            o.waits.append(dep)
            if not dep.is_dma:
                dep.signal = True
        for dep in o.waits:
            for k, v in dep.clock.items():
                if seen.get(k, 0) < v:
                    seen[k] = v
        clk = dict(seen)
        if dma:
            clk[("d", o.dsem)] = o.dval
        else:
            clk[eng] = o.pos
        o.clock = clk
        for r in reads:
            self.readers.setdefault(r, []).append(o)
        for r in writes:
            self.last_w[r] = o
            self.readers[r] = []
        self.ops[eng].append(o)
        if is_out:
            self.out_dmas.append(o)
        return o

    def emit(self):
        nc = self.nc
        with contextlib.ExitStack() as es:
            sems = {e: es.enter_context(nc.semaphore("s_" + e)) for e in COMPUTE}
            dsems = [es.enter_context(nc.semaphore("d%d" % i)) for i in range(N_DMA_SEMS + N_CC_SEMS)]
            for e in COMPUTE:
                c = 0
                for o in self.ops[e]:
                    if o.signal:
                        c += 1
                        o.sigval = c
            final_waits = list(self.out_dmas)
            block = es.enter_context(nc.Block())

            def run(engname):
                ops = self.ops[engname]

                def body(eng):
                    for o in ops:
                        for dep in o.waits:
                            if dep.is_dma:
                                eng.wait_ge(dsems[dep.dsem], dep.dval)
                            else:
                                eng.wait_ge(sems[dep.eng], dep.sigval)
                        ins = o.fn(eng)
                        if o.is_dma:
                            ins.then_inc(dsems[o.dsem], o.inc)
                        elif o.signal:
                            ins.then_inc(sems[o.eng], 1)
                    if engname == "sp":
                        for dep in final_waits:
                            eng.wait_ge(dsems[dep.dsem], dep.dval)
                return body

            block.tensor(run("pe"))
            block.scalar(run("act"))
            block.vector(run("dve"))
            block.gpsimd(run("pool"))
            block.sync(run("sp"))


class TokCtx:
    def __init__(self, nc, T, P=None, alloc=None, palloc=None, pt=None):
        self.nc = nc
        self.P = P if P is not None else Prog(nc)
        self.T = T
        self.H = min(512, T)
        self.NH = T // self.H
        a = alloc if alloc is not None else nc.alloc_sbuf_tensor
        self.alloc = a
        self.palloc = palloc if palloc is not None else a
        self.xs = a("xs", [128, KC, T], F32)
        self.hb = a("hb", [128, KC, T], BF16)
        self.ab = a("ab", [128, KC, T], BF16)
        self.wb = [a("wb%d" % i, [128, KC, 128], BF16) for i in range(3)]
        self.tmp = [a("tmp%d" % i, [128, T], F32) for i in range(3)]
        self.sqb = [a("sqb%d" % i, [128, T], BF16) for i in range(2)]
        self.rstd = a("rstd", [128, T], F32)
        self.ones = a("ones", [128, 128], BF16)
        self.pt = pt if pt is not None else [
            nc.alloc_psum_tensor("pt%d" % i, [128, T], F32) for i in range(4 if T > 512 else 6)]
        self.wi = 0
        self.pi = 0
        self.ti = 0
        self.si = 0
        self.uid = 0
        self.ones_d = a("ones_d", [128, 128], BF16)
        self.ones_h = a("ones_h", [128, 128], BF16)
        self.eps_c = a("eps_c", [128, 1], F32)
        self.P.op("dve", lambda e: e.memset(self.ones[:], 1.0), writes=["ones"])
        self.P.op("dve", lambda e: e.memset(self.ones_d[:], 1.0 / D), writes=["ones"])
        self.P.op("dve", lambda e: e.memset(self.ones_h[:], 1.0 / 128), writes=["ones"])
        self.P.op("dve", lambda e: e.memset(self.eps_c[:], EPS), writes=["eps_c"])

    def u(self):
        self.uid += 1
        return self.uid

    def next_pt(self):
        i = self.pi % len(self.pt)
        self.pi += 1
        return i

    def next_tmp(self):
        i = self.ti % len(self.tmp)
        self.ti += 1
        return i

    def next_sq(self):
        i = self.si % len(self.sqb)
        self.si += 1
        return i

    def load_vec(self, name, dram_ap, shape, dt=F32, sfx=""):
        t = self.palloc("sb_" + name + sfx, shape, dt)
        self.P.op("sp", lambda e: e.dma_start(out=t[:], in_=dram_ap), writes=[name], dma=True)
        return t

    def load_w(self, w_ap, kc=KC, ncols=128):
        s = self.wi % len(self.wb)
        self.wi += 1
        wb = self.wb[s]
        self.P.op("pool", lambda e: e.dma_start(
            out=wb[:, 0:kc, 0:ncols], in_=w_ap.rearrange("(kc p) j -> p kc j", p=128)),
            writes=[("wb", s)], dma=True)
        return s

    def mm(self, s, rhs_fn, rkeys, kc=KC, m=128):
        P = self.P
        p = self.next_pt()
        pt = self.pt[p]
        wb = self.wb[s]
        H = self.H
        for k in range(kc):
            for h in range(self.NH):
                P.op("pe", lambda e, k=k, h=h: e.matmul(
                    pt[0:m, h * H:(h + 1) * H], wb[:, k, 0:m], rhs_fn(k, h * H, (h + 1) * H),
                    start=(k == 0), stop=(k == kc - 1)),
                    reads=[("wb", s), rkeys(k)], writes=[("pt", p)])
        return p

    def ssq_ps(self, srcs, rkeys, nparts=128):
        P = self.P
        onesm = self.ones_d if len(srcs) == KC else self.ones_h
        H = self.H
        p = self.next_pt()
        pt = self.pt[p]
        n = len(srcs)
        for i, src in enumerate(srcs):
            q = self.next_sq()
            sq = self.sqb[q]
            P.op("act", lambda e, src=src, sq=sq: e.activation(sq[0:nparts, :], src, AF.Square),
                 reads=[rkeys(i)], writes=[("sq", q)])
            for h in range(self.NH):
                P.op("pe", lambda e, h=h, sq=sq, i=i: e.matmul(
                    pt[:, h * H:(h + 1) * H], onesm[0:nparts, :], sq[0:nparts, h * H:(h + 1) * H],
                    start=(i == 0), stop=(i == n - 1)),
                    reads=["ones", ("sq", q)], writes=[("pt", p)])
        return p

    def rstd_from(self, p, n_elems, out_key="rstd"):
        P = self.P
        pt = self.pt[p]
        P.op("act", lambda e: e.activation(self.rstd[:], pt[:], AF.Ln, bias=self.eps_c[:]),
             reads=[("pt", p), "eps_c"], writes=[out_key])
        P.op("act", lambda e: e.activation(self.rstd[:], self.rstd[:], AF.Exp, scale=-0.5),
             reads=[out_key], writes=[out_key])

    def norm_x(self, g, gname):
        P = self.P
        p = self.ssq_ps([self.xs[:, k, :] for k in range(KC)], lambda i: ("xs", i))
        self.rstd_from(p, D)
        for k in range(KC):
            P.op("dve", lambda e, k=k: e.scalar_tensor_tensor(
                self.hb[:, k, :], self.xs[:, k, :], g[:, k:k + 1], self.rstd[:], ALU.mult, ALU.mult),
                reads=[("xs", k), gname, "rstd"], writes=[("hb", k)])

    def hb_rhs(self):
        return (lambda k, a, b: self.hb[:, k, a:b]), (lambda k: ("hb", k))

    def ab_rhs(self):
        return (lambda k, a, b: self.ab[:, k, a:b]), (lambda k: ("ab", k))

    def ffn(self, g, gname, w_gu, w_down):
        P = self.P
        self.norm_x(g, gname)
        hr, hk = self.hb_rhs()
        ar, ak = self.ab_rhs()
        for half in range(2):
            for j in range(KC):
                hc = half * KC + j
                sg_ = self.load_w(w_gu[:, hc * 128:(hc + 1) * 128])
                pg = self.mm(sg_, hr, hk)
                su_ = self.load_w(w_gu[:, 4096 + hc * 128:4096 + (hc + 1) * 128])
                pu = self.mm(su_, hr, hk)
                t = self.next_tmp()
                P.op("act", lambda e, pg=pg, t=t: e.activation(self.tmp[t][:], self.pt[pg][:], AF.Silu),
                     reads=[("pt", pg)], writes=[("tmp", t)])
                P.op("dve", lambda e, pu=pu, t=t, j=j: e.tensor_tensor(
                    self.ab[:, j, :], self.tmp[t][:], self.pt[pu][:], ALU.mult),
                    reads=[("tmp", t), ("pt", pu)], writes=[("ab", j)])
            for oc in range(KC):
                s = self.load_w(w_down[half * 2048:(half + 1) * 2048, oc * 128:(oc + 1) * 128])
                p = self.mm(s, ar, ak)
                P.op("dve", lambda e, p=p, oc=oc: e.scalar_tensor_tensor(
                    self.xs[:, oc, :], self.pt[p][:], 0.5, self.xs[:, oc, :], ALU.mult, ALU.add),
                    reads=[("pt", p), ("xs", oc)], writes=[("xs", oc)])

    def proj_add(self, w, rhs, rk, kc):
        P = self.P
        for oc in range(KC):
            s = self.load_w(w[:, oc * 128:(oc + 1) * 128], kc=kc)
            p = self.mm(s, rhs, rk, kc=kc)
            P.op("dve", lambda e, p=p, oc=oc: e.tensor_tensor(
                self.xs[:, oc, :], self.pt[p][:], self.xs[:, oc, :], ALU.add),
                reads=[("pt", p), ("xs", oc)], writes=[("xs", oc)])

    def load_x(self, xT, t0):
        T = self.T
        for k in range(KC):
            self.P.op("sp", lambda e, k=k: e.dma_start(
                out=self.xs[:, k, :], in_=xT[k * 128:(k + 1) * 128, t0:t0 + T]),
                writes=[("xs", k)], dma=True)

    def store_x(self, outT, t0):
        T = self.T
        for k in range(KC):
            self.P.op("sp", lambda e, k=k: e.dma_start(
                out=outT[k * 128:(k + 1) * 128, t0:t0 + T], in_=self.xs[:, k, :]),
                reads=[("xs", k)], dma=True, is_out=True)

    def ps_to_dram(self, p, dst_ap, m=128, i=0):
        P = self.P
        t = self.next_tmp()
        if i % 2 == 0:
            P.op("act", lambda e: e.copy(self.tmp[t][0:m, :], self.pt[p][0:m, :]),
                 reads=[("pt", p)], writes=[("tmp", t)])
        else:
            P.op("dve", lambda e: e.tensor_copy(self.tmp[t][0:m, :], self.pt[p][0:m, :]),
                 reads=[("pt", p)], writes=[("tmp", t)])
        P.op("sp", lambda e: e.dma_start(out=dst_ap, in_=self.tmp[t][0:m, :]),
             reads=[("tmp", t)], dma=True, is_out=True)

    def xattn(self, g, gname, w_q, gq, w_o, knT, vm):
        P = self.P
        T = self.T
        H = self.H
        self.norm_x(g, gname)
        hr, hk = self.hb_rhs()
        for hd in range(4):
            s = self.load_w(w_q[:, hd * 128:(hd + 1) * 128])
            pq = self.mm(s, hr, hk)
            p2 = self.ssq_ps([self.pt[pq][:]], lambda i: ("pt", pq))
            self.rstd_from(p2, 128)
            P.op("dve", lambda e, pq=pq: e.scalar_tensor_tensor(
                self.sqb[0][:], self.pt[pq][:], gq[:, 0:1], self.rstd[:], ALU.mult, ALU.mult),
                reads=[("pt", pq), "xa_gq", "rstd"], writes=[("sq", 0)])
            pbs = []
            for mc in range(2):
                ps_ = self.next_pt()
                for h in range(self.NH):
                    P.op("pe", lambda e, mc=mc, h=h, ps_=ps_, hd=hd: e.matmul(
                        self.pt[ps_][:, h * H:(h + 1) * H], knT[:, hd, mc * 128:(mc + 1) * 128],
                        self.sqb[0][:, h * H:(h + 1) * H], start=True, stop=True),
                        reads=["knT", ("sq", 0)], writes=[("pt", ps_)])
                pb = ("xp", mc)
                P.op("act", lambda e, mc=mc, ps_=ps_: e.activation(
                    self.pexp[mc][:], self.pt[ps_][:], AF.Exp),
                    reads=[("pt", ps_)], writes=[pb])
                pbs.append(pb)
            pden = self.next_pt()
            po = self.next_pt()
            for mc in range(2):
                for h in range(self.NH):
                    P.op("pe", lambda e, mc=mc, h=h, pden=pden: e.matmul(
                        self.pt[pden][:, h * H:(h + 1) * H], self.ones[:], self.pexp[mc][:, h * H:(h + 1) * H],
                        start=(mc == 0), stop=(mc == 1)),
                        reads=["ones", ("xp", mc)], writes=[("pt", pden)])
            for mc in range(2):
                for h in range(self.NH):
                    P.op("pe", lambda e, mc=mc, h=h, hd=hd, po=po: e.matmul(
                        self.pt[po][:, h * H:(h + 1) * H], vm[:, mc, hd * 128:(hd + 1) * 128],
                        self.pexp[mc][:, h * H:(h + 1) * H], start=(mc == 0), stop=(mc == 1)),
                        reads=["vm", ("xp", mc)], writes=[("pt", po)])
            t = self.next_tmp()
            P.op("dve", lambda e, t=t, pden=pden: e.reciprocal(self.tmp[t][:], self.pt[pden][:]),
                 reads=[("pt", pden)], writes=[("tmp", t)])
            P.op("dve", lambda e, t=t, hd=hd, po=po: e.tensor_tensor(
                self.ab[:, hd, :], self.pt[po][:], self.tmp[t][:], ALU.mult),
                reads=[("pt", po), ("tmp", t)], writes=[("ab", hd)])
        ar, ak = self.ab_rhs()
        self.proj_add(w_o, ar, ak, kc=4)

    def mem_kv(self, memT, g_mem, w_kv, gk, sfx=""):
        nc, P = self.nc, self.P
        M = 256
        knT = self.palloc("knT" + sfx, [128, 4, M], BF16)
        vm = self.palloc("vm" + sfx, [128, 2, 512], BF16)
        if not hasattr(self, "pexp"):
            self.pexp = [self.alloc("pexp%d" % i, [128, self.T], BF16) for i in range(2)]
        for k in range(KC):
            P.op("sp", lambda e, k=k: e.dma_start(out=self.xs[:, k, 0:M], in_=memT[k * 128:(k + 1) * 128, :]),
                 writes=[("xs", k)], dma=True)
        p = self.next_pt()
        pt = self.pt[p]
        for k in range(KC):
            q = self.next_sq()
            sq = self.sqb[q]
            P.op("act", lambda e, k=k, sq=sq: e.activation(sq[:, 0:M], self.xs[:, k, 0:M], AF.Square),
                 reads=[("xs", k)], writes=[("sq", q)])
            P.op("pe", lambda e, k=k, sq=sq: e.matmul(pt[:, 0:M], self.ones_d[:], sq[:, 0:M],
                                                     start=(k == 0), stop=(k == KC - 1)),
                 reads=["ones", ("sq", q)], writes=[("pt", p)])
        P.op("act", lambda e: e.activation(self.rstd[:, 0:M], pt[:, 0:M], AF.Ln, bias=self.eps_c[:]),
             reads=[("pt", p), "eps_c"], writes=["rstd"])
        P.op("act", lambda e: e.activation(self.rstd[:, 0:M], self.rstd[:, 0:M], AF.Exp, scale=-0.5),
             reads=["rstd"], writes=["rstd"])
        for k in range(KC):
            P.op("dve", lambda e, k=k: e.scalar_tensor_tensor(
                self.hb[:, k, 0:M], self.xs[:, k, 0:M], g_mem[:, k:k + 1], self.rstd[:, 0:M], ALU.mult, ALU.mult),
                reads=[("xs", k), "g_mem", "rstd"], writes=[("hb", k)])
        for hd in range(4):
            s = self.load_w(w_kv[:, hd * 128:(hd + 1) * 128])
            pk = self.next_pt()
            for k in range(KC):
                P.op("pe", lambda e, k=k, s=s, pk=pk: e.matmul(
                    self.pt[pk][:, 0:M], self.wb[s][:, k, :], self.hb[:, k, 0:M],
                    start=(k == 0), stop=(k == KC - 1)),
                    reads=[("wb", s), ("hb", k)], writes=[("pt", pk)])
            q = self.next_sq()
            sq = self.sqb[q]
            P.op("act", lambda e, sq=sq, pk=pk: e.activation(sq[:, 0:M], self.pt[pk][:, 0:M], AF.Square),
                 reads=[("pt", pk)], writes=[("sq", q)])
            p2 = self.next_pt()
            P.op("pe", lambda e, sq=sq, p2=p2: e.matmul(self.pt[p2][:, 0:M], self.ones_h[:], sq[:, 0:M],
                                                       start=True, stop=True),
                 reads=["ones", ("sq", q)], writes=[("pt", p2)])
            P.op("act", lambda e, p2=p2: e.activation(
                self.rstd[:, 0:M], self.pt[p2][:, 0:M], AF.Ln, bias=self.eps_c[:]),
                reads=[("pt", p2), "eps_c"], writes=["rstd"])
            P.op("act", lambda e: e.activation(self.rstd[:, 0:M], self.rstd[:, 0:M], AF.Exp, scale=-0.5),
                 reads=["rstd"], writes=["rstd"])
            P.op("dve", lambda e, hd=hd, pk=pk: e.scalar_tensor_tensor(
                knT[:, hd, :], self.pt[pk][:, 0:M], gk[:, 0:1], self.rstd[:, 0:M], ALU.mult, ALU.mult),
                reads=[("pt", pk), "xa_gk", "rstd"], writes=["knT"])
        for hd in range(4):
            s = self.load_w(w_kv[:, 512 + hd * 128:512 + (hd + 1) * 128])
            for mc in range(2):
                pv = self.next_pt()
                for k in range(KC):
                    P.op("pe", lambda e, k=k, mc=mc, pv=pv, s=s: e.matmul(
                        self.pt[pv][:, 0:128], self.hb[:, k, mc * 128:(mc + 1) * 128], self.wb[s][:, k, :],
                        start=(k == 0), stop=(k == KC - 1)),
                        reads=[("wb", s), ("hb", k)], writes=[("pt", pv)])
                P.op("act", lambda e, mc=mc, pv=pv, hd=hd: e.copy(
                    vm[:, mc, hd * 128:(hd + 1) * 128], self.pt[pv][:, 0:128]),
                    reads=[("pt", pv)], writes=["vm"])
        return knT, vm


def _col(a):
    return np.ascontiguousarray(np.asarray(a, np.float32).reshape(KC, 128).T)


def _dram_in(nc, name, shape, dt=F32):
    return nc.dram_tensor(name, list(shape), dt, kind="ExternalInput").ap()


def _dram_out(nc, name, shape, dt=F32):
    return nc.dram_tensor(name, list(shape), dt, kind="ExternalOutput").ap()


AB_W = 6160


def build_l1(TOK, T):
    nc = bass.Bass("TRN2", target_bir_lowering=False)
    xT = _dram_in(nc, "xT", [D, TOK])
    g1 = _dram_in(nc, "g1", [128, KC])
    w_gu = _dram_in(nc, "w_gu", [D, 8192])
    w_dn = _dram_in(nc, "w_dn", [4096, D])
    gm = _dram_in(nc, "gm", [128, KC])
    w_in = _dram_in(nc, "w_in", [D, AB_W])
    gq = _dram_in(nc, "gq", [128, 1])
    gk = _dram_in(nc, "gk", [128, 1])
    x1T = _dram_out(nc, "x1T", [D, TOK])
    projT = _dram_out(nc, "projT", [AB_W, TOK])
    c = TokCtx(nc, T)
    P = c.P
    g1s = c.load_vec("g1s", g1, [128, KC])
    gms = c.load_vec("gms", gm, [128, KC])
    gqs = c.load_vec("gqs", gq, [128, 1])
    gks = c.load_vec("gks", gk, [128, 1])
    P.op("dve", lambda e: e.tensor_scalar_mul(gqs[:], gqs[:], 128.0 ** -0.5),
         reads=["gqs"], writes=["gqs"])
    hr, hk = c.hb_rhs()
    for ti in range(TOK // T):
        t0 = ti * T
        c.load_x(xT, t0)
        c.ffn(g1s, "g1s", w_gu, w_dn)
        c.store_x(x1T, t0)
        c.norm_x(gms, "gms")
        for oc in range(48):
            s = c.load_w(w_in[:, oc * 128:(oc + 1) * 128])
            p = c.mm(s, hr, hk)
            dst = projT[oc * 128:(oc + 1) * 128, t0:t0 + T]
            if oc < 8:
                p2 = c.ssq_ps([c.pt[p][:]], lambda i, p=p: ("pt", p))
                c.rstd_from(p2, 128)
                t = c.next_tmp()
                gs, gn = (gqs, "gqs") if oc < 4 else (gks, "gks")
                P.op("dve", lambda e, p=p, t=t, gs=gs: e.scalar_tensor_tensor(
                    c.tmp[t][:], c.pt[p][:], gs[:, 0:1], c.rstd[:], ALU.mult, ALU.mult),
                    reads=[("pt", p), gn, "rstd"], writes=[("tmp", t)])
                P.op("sp", lambda e, t=t, dst=dst: e.dma_start(out=dst, in_=c.tmp[t][:]),
                     reads=[("tmp", t)], dma=True, is_out=True)
            else:
                c.ps_to_dram(p, dst, i=oc)
        s = c.load_w(w_in[:, 6144:6160], ncols=16)
        p = c.mm(s, hr, hk, m=16)
        c.ps_to_dram(p, projT[6144:6160, t0:t0 + T], m=16)
    P.emit()
    return nc


def build_l2(S):
    nc = bass.Bass("TRN2", target_bir_lowering=False)
    NB = S // 128
    NG = S // 512
    qn = _dram_in(nc, "qn", [128, S])
    kn = _dram_in(nc, "kn", [128, S])
    vs = _dram_in(nc, "vs", [S, 128])
    Lm = _dram_in(nc, "Lm", [128, 128])
    Um = _dram_in(nc, "Um", [128, 128])
    U16 = _dram_in(nc, "U16", [128, 128])
    SU16 = _dram_in(nc, "SU16", [128, 128])
    msk = _dram_in(nc, "msk", [4, 128, 512])
    gq = _dram_in(nc, "gqT", [3, 64, S])
    gk = _dram_in(nc, "gkT", [3, 64, S])
    gkt = _dram_in(nc, "gkt", [3, S, 64])
    gvt = _dram_in(nc, "gvt", [3, S, 128])
    glr = _dram_in(nc, "glr", [16, S])
    wga = _dram_in(nc, "wga", [32, 192])
    osb = _dram_out(nc, "osb", [128, S])
    ogl = _dram_out(nc, "ogl", [3, 128, S])
    P = Prog(nc)
    a = nc.alloc_sbuf_tensor
    qs = a("qs", [128, S], BF16)
    ks = a("ks", [128, S], BF16)
    vsb = a("vsb", [128, NB, 128], BF16)
    Ls = a("Ls", [128, 128], BF16)
    ones = a("ones", [128, 128], BF16)
    mk = a("mk", [128, 4, 512], F32)
    CH = 2048 if S >= 2048 else S
    for i in range(S // CH):
        P.op("pool", lambda e, i=i: e.dma_start(out=qs[:, i * CH:(i + 1) * CH], in_=qn[:, i * CH:(i + 1) * CH]),
             writes=[("qs", i)], dma=True)
        P.op("pool", lambda e, i=i: e.dma_start(out=ks[:, i * CH:(i + 1) * CH], in_=kn[:, i * CH:(i + 1) * CH]),
             writes=[("ks", i)], dma=True)
        nb = CH // 128
        P.op("pool", lambda e, i=i, nb=nb: e.dma_start(
            out=vsb[:, i * nb:(i + 1) * nb, :],
            in_=vs[i * CH:(i + 1) * CH, :].rearrange("(n p) e -> p n e", p=128)),
            writes=[("vsb", i)], dma=True)
    P.op("pool", lambda e: e.dma_start(out=Ls[:], in_=Lm), writes=["Ls"], dma=True)
    P.op("sp", lambda e: e.dma_start(out=mk[:], in_=msk.rearrange("r p t -> p r t")), writes=["mk"], dma=True)
    P.op("dve", lambda e: e.memset(ones[:], 1.0), writes=["ones"])
    one_c = a("one_c", [128, 1], F32)
    P.op("dve", lambda e: e.memset(one_c[:], 1.0), writes=["one_c"])
    NZ = 3
    zp = [nc.alloc_psum_tensor("zp%d" % i, [128, 512], F32) for i in range(NZ)]
    cp = [nc.alloc_psum_tensor("cp%d" % i, [128, 512], F32) for i in range(2)]
    op_ = [nc.alloc_psum_tensor("op%d" % i, [128, 512], F32) for i in range(2)]
    gp = nc.alloc_psum_tensor("gp", [128, 512], F32)
    e1 = [a("e1_%d" % i, [128, 512], F32) for i in range(2)]
    spb = [a("spb%d" % i, [128, 512], BF16) for i in range(3)]
    tl = [a("tl%d" % i, [128, 512], F32) for i in range(2)]
    wbf = [a("wbf%d" % i, [128, 512], BF16) for i in range(3)]
    Rb = [a("Rb%d" % i, [128, 512], F32) for i in range(2)]
    ost = [a("ost%d" % i, [128, 512], F32) for i in range(2)]

    steps = []
    for G in range(NG):
        for n in range(4 * G + 3, -1, -1):
            steps.append((G, n))
    cnt = {"z": 0, "sp": 0, "c": 0, "w": 0}
    st = {}

    def stage_a(i):
        G, n = steps[i]
        zi = i % NZ
        ei = i % 2
        si = i % 3
        st[i] = (zi, ei, si)
        qc = (G * 512) // CH
        kc_ = (n * 128) // CH
        P.op("pe", lambda e: e.matmul(zp[zi][:], ks[:, n * 128:(n + 1) * 128], qs[:, G * 512:(G + 1) * 512],
                                      start=True, stop=True),
             reads=[("ks", kc_), ("qs", qc)], writes=[("zp", zi)])
        P.op("act", lambda e: e.activation(e1[ei][:], zp[zi][:], AF.Exp),
             reads=[("zp", zi)], writes=[("e1", ei)])
        r = n - 4 * G
        if r >= 0:
            P.op("act", lambda e: e.activation(e1[ei][:], e1[ei][:], AF.Ln, bias=one_c[:]),
                 reads=[("e1", ei)], writes=[("e1", ei)])
            P.op("pool", lambda e: e.tensor_tensor(spb[si][:], e1[ei][:], mk[:, r, :], ALU.mult),
                 reads=[("e1", ei), "mk"], writes=[("spb", si)])
        else:
            P.op("act", lambda e: e.activation(spb[si][:], e1[ei][:], AF.Ln, bias=one_c[:]),
                 reads=[("e1", ei)], writes=[("spb", si)])

    def stage_b(i):
        G, n = steps[i]
        zi, ei, si = st[i]
        ci = i % 2
        ti = i % 2
        wi = i % 3
        rb = G % 2
        first = (n == 4 * G + 3)
        P.op("pe", lambda e: e.matmul(zp[zi][:], Ls[:], spb[si][:], start=False, stop=True),
             reads=["Ls", ("spb", si)], writes=[("zp", zi)])
        P.op("pe", lambda e: e.matmul(cp[ci][:], ones[:], spb[si][:], start=True, stop=True),
             reads=["ones", ("spb", si)], writes=[("cp", ci)])
        if first:
            P.op("dve", lambda e: e.memset(Rb[rb][:], 0.0), writes=[("Rb", rb)])
        P.op("dve", lambda e: e.tensor_tensor(tl[ti][:], zp[zi][:], Rb[rb][:], ALU.subtract),
             reads=[("zp", zi), ("Rb", rb)], writes=[("tl", ti)])
        r = n - 4 * G
        if r >= 0:
            P.op("act", lambda e: e.activation(tl[ti][:], tl[ti][:], AF.Exp),
                 reads=[("tl", ti)], writes=[("tl", ti)])
            P.op("pool", lambda e: e.tensor_tensor(wbf[wi][:], tl[ti][:], mk[:, r, :], ALU.mult),
                 reads=[("tl", ti), "mk"], writes=[("wbf", wi)])
        else:
            P.op("act", lambda e: e.activation(wbf[wi][:], tl[ti][:], AF.Exp),
                 reads=[("tl", ti)], writes=[("wbf", wi)])
        P.op("dve", lambda e: e.tensor_tensor(Rb[rb][:], Rb[rb][:], cp[ci][:], ALU.add),
             reads=[("Rb", rb), ("cp", ci)], writes=[("Rb", rb)])

    def stage_c(i):
        G, n = steps[i]
        wi = i % 3
        oi = G % 2
        vc = (n * 128) // CH
        P.op("pe", lambda e: e.matmul(op_[oi][:], vsb[:, n, :], wbf[wi][:],
                                      start=(n == 4 * G + 3), stop=(n == 0)),
             reads=[("vsb", vc), ("wbf", wi)], writes=[("op", oi)])
        if n == 0:
            P.op("act", lambda e: e.copy(ost[oi][:], op_[oi][:]), reads=[("op", oi)], writes=[("ost", oi)])
            P.op("sp", lambda e: e.dma_start(out=osb[:, G * 512:(G + 1) * 512], in_=ost[oi][:]),
                 reads=[("ost", oi)], dma=True, is_out=True)

    ns = len(steps)
    for i in range(ns + 2):
        if i < ns:
            stage_a(i)
        if 0 <= i - 1 < ns:
            stage_b(i - 1)
        if 0 <= i - 2 < ns:
            stage_c(i - 2)

    P.barrier()
    Us = a("Us", [128, 128], F32)
    U16s = a("U16s", [128, 128], F32)
    SU16s = a("SU16s", [128, 128], F32)
    wgs = a("wgs", [32, 192], F32)
    P.op("sp", lambda e: e.dma_start(out=Us[:], in_=Um), writes=["Us"], dma=True)
    P.op("sp", lambda e: e.dma_start(out=U16s[:], in_=U16), writes=["U16s"], dma=True)
    P.op("sp", lambda e: e.dma_start(out=SU16s[:], in_=SU16), writes=["SU16s"], dma=True)
    P.op("sp", lambda e: e.dma_start(out=wgs[:], in_=wga), writes=["wgs"], dma=True)
    GC = 256 if S >= 256 else S
    NCH = GC // 128
    glrs = [a("glrs%d" % i, [32, GC], F32) for i in range(2)]
    qts = [a("qts%d" % i, [64, 3, GC], F32) for i in range(2)]
    kts = [a("kts%d" % i, [64, 3, GC], F32) for i in range(2)]
    ktk = [a("ktk%d" % i, [128, 3, NCH, 64], F32) for i in range(2)]
    vtk = [a("vtk%d" % i, [128, 3, NCH, 128], BF16) for i in range(2)]
    ogs = [a("ogs%d" % i, [128, 3, GC], F32) for i in range(2)]
    state = [a("state%d" % h, [64, 128], F32) for h in range(3)]
    state_b = [a("stateb%d" % h, [64, 128], BF16) for h in range(3)]
    NR = 3
    gab = [a("gab%d" % i, [128, 64], F32) for i in range(NR)]
    gex = [a("gex%d" % i, [128, 64], F32) for i in range(NR)]
    gmn = [a("gmn%d" % i, [128, 64], F32) for i in range(NR)]
    ls16 = [a("ls16_%d" % i, [128, 64], F32) for i in range(NR)]
    E1 = [a("E1_%d" % i, [64, 128], F32) for i in range(NR)]
    E2 = [a("E2_%d" % i, [64, 128], F32) for i in range(NR)]
    E3 = [a("E3_%d" % i, [128, 64], F32) for i in range(NR)]
    ebl = [a("ebl%d" % i, [64, 1], F32) for i in range(NR)]
    qtl = [a("qtl%d" % i, [64, 128], BF16) for i in range(NR)]
    ktl = [a("ktl%d" % i, [64, 128], BF16) for i in range(NR)]
    khat = [a("khat%d" % i, [128, 64], BF16) for i in range(NR)]
    Asb = [a("Asb%d" % i, [128, 128], BF16) for i in range(NR)]
    for h in range(3):
        P.op("dve", lambda e, h=h: e.memset(state[h][:], 0.0), writes=[("state", h)])
        P.op("dve", lambda e, h=h: e.memset(state_b[h][:], 0.0), writes=[("stateb", h)])
    for i in range(2):
        P.op("dve", lambda e, i=i: e.memset(glrs[i][:], 1.0), writes=[("glrs", i)])
    g_ps = [gp[:, r * 64:(r + 1) * 64] for r in range(NR)]
    ct_ps = [gp[:, 192 + r * 64:192 + (r + 1) * 64] for r in range(NR)]
    bt_ps = [cp[0][0:64, r * 128:(r + 1) * 128] for r in range(NR)]
    sc_ps = [cp[1][:, r * 128:(r + 1) * 128] for r in range(NR)]
    o_ps = [zp[0][:, r * 128:(r + 1) * 128] for r in range(NR)]
    kv_ps = [zp[1][0:64, r * 128:(r + 1) * 128] for r in range(NR)]
    items = []
    for g in range(S // GC):
        for c in range(NCH):
            for h in range(3):
                items.append((g, c, h))
    NI = len(items)

    def loads(g):
        b = g % 2
        t0 = g * GC
        P.op("sp", lambda e: e.dma_start(out=glrs[b][0:16, :], in_=glr[:, t0:t0 + GC]),
             writes=[("glrs", b)], dma=True)
        P.op("sp", lambda e: e.dma_start(
            out=qts[b][:], in_=gq[:, :, t0:t0 + GC].rearrange("h d t -> d h t")), writes=[("qts", b)], dma=True)
        P.op("sp", lambda e: e.dma_start(
            out=kts[b][:], in_=gk[:, :, t0:t0 + GC].rearrange("h d t -> d h t")), writes=[("kts", b)], dma=True)
        for h in range(3):
            P.op("sp", lambda e, h=h: e.dma_start(
                out=ktk[b][:, h, :, :], in_=gkt[h, t0:t0 + GC, :].rearrange("(n p) d -> p n d", p=128)),
                writes=[("ktk", b, h)], dma=True)
            P.op("pool", lambda e, h=h: e.dma_start(
                out=vtk[b][:, h, :, :], in_=gvt[h, t0:t0 + GC, :].rearrange("(n p) d -> p n d", p=128)),
                writes=[("vtk", b, h)], dma=True)

    def f1(i):
        g, c, h = items[i]
        if c == 0 and h == 0:
            loads(g)
        b = g % 2
        r = i % NR
        cs = slice(c * 128, (c + 1) * 128)
        rk = lambda nm: (nm, r)
        P.op("pe", lambda e: e.matmul(g_ps[r], glrs[b][:, cs], wgs[:, h * 64:(h + 1) * 64], start=True, stop=True),
             reads=[("glrs", b), "wgs"], writes=[rk("g_ps")])
        P.op("dve", lambda e: e.tensor_scalar_min(gmn[r][:], g_ps[r], 0.0),
             reads=[rk("g_ps")], writes=[rk("gmn")])
        P.op("dve", lambda e: e.scalar_tensor_tensor(gab[r][:], g_ps[r], 0.0, gmn[r][:], ALU.max, ALU.subtract),
             reads=[rk("g_ps"), rk("gmn")], writes=[rk("gab")])
        P.op("act", lambda e: e.activation(gex[r][:], gab[r][:], AF.Exp, scale=-1.0),
             reads=[rk("gab")], writes=[rk("gex")])
        P.op("act", lambda e: e.activation(gex[r][:], gex[r][:], AF.Ln, bias=one_c[:]),
             reads=[rk("gex")], writes=[rk("gex")])
        P.op("dve", lambda e: e.tensor_tensor(ls16[r][:], gmn[r][:], gex[r][:], ALU.subtract),
             reads=[rk("gmn"), rk("gex")], writes=[rk("ls16")])

    def f2(i):
        g, c, h = items[i]
        b = g % 2
        r = i % NR
        cs = slice(c * 128, (c + 1) * 128)
        rk = lambda nm: (nm, r)
        P.op("pe", lambda e: e.matmul(bt_ps[r], ls16[r][:], U16s[:], start=True, stop=True),
             reads=[rk("ls16"), "U16s"], writes=[rk("bt_ps")])
        P.op("pe", lambda e: e.matmul(ct_ps[r], SU16s[:], ls16[r][:], start=True, stop=True),
             reads=[rk("ls16"), "SU16s"], writes=[rk("ct_ps")])
        P.op("act", lambda e: e.activation(E1[r][:], bt_ps[r], AF.Exp),
             reads=[rk("bt_ps")], writes=[rk("E1")])
        P.op("act", lambda e: e.activation(E2[r][:], bt_ps[r], AF.Exp, scale=-1.0),
             reads=[rk("bt_ps")], writes=[rk("E2")])
        P.op("act", lambda e: e.activation(E3[r][:], ct_ps[r], AF.Exp),
             reads=[rk("ct_ps")], writes=[rk("E3")])
        P.op("act", lambda e: e.activation(ebl[r][:], bt_ps[r][:, 127:128], AF.Exp),
             reads=[rk("bt_ps")], writes=[rk("ebl")])
        P.op("dve", lambda e: e.scalar_tensor_tensor(
            qtl[r][:], qts[b][:, h, cs], 0.125, E1[r][:], ALU.mult, ALU.mult),
            reads=[("qts", b), rk("E1")], writes=[rk("qtl")])
        P.op("dve", lambda e: e.tensor_tensor(ktl[r][:], kts[b][:, h, cs], E2[r][:], ALU.mult),
             reads=[("kts", b), rk("E2")], writes=[rk("ktl")])
        P.op("pool", lambda e: e.tensor_tensor(khat[r][:], ktk[b][:, h, c, :], E3[r][:], ALU.mult),
             reads=[("ktk", b, h), rk("E3")], writes=[rk("khat")])

    def bk(i):
        g, c, h = items[i]
        b = g % 2
        t0 = g * GC
        r = i % NR
        cs = slice(c * 128, (c + 1) * 128)
        rk = lambda nm: (nm, r)
        P.op("pe", lambda e: e.matmul(sc_ps[r], ktl[r][:], qtl[r][:], start=True, stop=True),
             reads=[rk("ktl"), rk("qtl")], writes=[rk("sc_ps")])
        P.op("dve", lambda e: e.tensor_tensor(Asb[r][:], sc_ps[r], Us[:], ALU.mult),
             reads=[rk("sc_ps"), "Us"], writes=[rk("Asb")])
        P.op("pe", lambda e: e.matmul(o_ps[r], vtk[b][:, h, c, :], Asb[r][:], start=True, stop=False),
             reads=[("vtk", b, h), rk("Asb")], writes=[rk("o_ps")])
        P.op("pe", lambda e: e.matmul(o_ps[r], state_b[h][:], qtl[r][:], start=False, stop=True),
             reads=[("stateb", h), rk("qtl")], writes=[rk("o_ps")])
        P.op("act", lambda e: e.copy(ogs[b][:, h, cs], o_ps[r]),
             reads=[rk("o_ps")], writes=[("ogs", b, h, c)])
        P.op("pe", lambda e: e.matmul(kv_ps[r], khat[r][:], vtk[b][:, h, c, :], start=True, stop=True),
             reads=[rk("khat"), ("vtk", b, h)], writes=[rk("kv_ps")])
        P.op("dve", lambda e: e.scalar_tensor_tensor(
            state[h][:], state[h][:], ebl[r][:, 0:1], kv_ps[r], ALU.mult, ALU.add),
            reads=[("state", h), rk("ebl"), rk("kv_ps")], writes=[("state", h)])
        P.op("pool", lambda e: e.tensor_copy(state_b[h][:], state[h][:]),
             reads=[("state", h)], writes=[("stateb", h)])
        if c == NCH - 1 and h == 2:
            P.op("sp", lambda e: e.dma_start(
                out=ogl[:, :, t0:t0 + GC].rearrange("h e t -> e h t"), in_=ogs[b][:]),
                reads=[("ogs", b, hh, cc) for hh in range(3) for cc in range(NCH)], dma=True, is_out=True)

    for i in range(NI + 2):
        if i < NI:
            f1(i)
        if 0 <= i - 1 < NI:
            f2(i - 1)
        if 0 <= i - 2 < NI:
            bk(i - 2)
    P.emit()
    return nc


def _xa_inputs(nc, sfx):
    return dict(
        g_xa=_dram_in(nc, "g_xa" + sfx, [128, KC]), memT=_dram_in(nc, "memT" + sfx, [D, 256]),
        g_mem=_dram_in(nc, "g_mem" + sfx, [128, KC]), w_q=_dram_in(nc, "w_q" + sfx, [D, 512]),
        w_kv=_dram_in(nc, "w_kv" + sfx, [D, 1024]), gq=_dram_in(nc, "xgq" + sfx, [128, 1]),
        gk=_dram_in(nc, "xgk" + sfx, [128, 1]), w_o=_dram_in(nc, "w_o" + sfx, [512, D]))


def _xa_setup(c, xa, sfx=""):
    P = c.P
    xa["g_xa_s"] = c.load_vec("g_xa_s", xa["g_xa"], [128, KC], sfx=sfx)
    g_mem_s = c.load_vec("g_mem", xa["g_mem"], [128, KC], sfx=sfx)
    xa["gq_s"] = c.load_vec("xa_gq", xa["gq"], [128, 1], sfx=sfx)
    gk_s = c.load_vec("xa_gk", xa["gk"], [128, 1], sfx=sfx)
    P.op("dve", lambda e: e.tensor_scalar_mul(xa["gq_s"][:], xa["gq_s"][:], 128.0 ** -0.5),
         reads=["xa_gq"], writes=["xa_gq"])
    xa["knT"], xa["vm"] = c.mem_kv(xa["memT"], g_mem_s, xa["w_kv"], gk_s, sfx=sfx)


def _xa_run(c, xa):
    c.xattn(xa["g_xa_s"], "g_xa_s", xa["w_q"], xa["gq_s"], xa["w_o"], xa["knT"], xa["vm"])


def build_l3(TOK, T):
    nc = bass.Bass("TRN2", target_bir_lowering=False)
    x1T = _dram_in(nc, "x1T", [D, TOK])
    osbT = _dram_in(nc, "osbT", [512, TOK])
    oglT = _dram_in(nc, "oglT", [1536, TOK])
    rT = _dram_in(nc, "rT", [1536, TOK])
    g_og = _dram_in(nc, "g_og", [128, 1])
    w_out = _dram_in(nc, "w_out", [D, D])
    xa = _xa_inputs(nc, "")
    g2 = _dram_in(nc, "g2", [128, KC])
    w_gu2 = _dram_in(nc, "w_gu2", [D, 8192])
    w_dn2 = _dram_in(nc, "w_dn2", [4096, D])
    g1b = _dram_in(nc, "g1b", [128, KC])
    w_gu1b = _dram_in(nc, "w_gu1b", [D, 8192])
    w_dn1b = _dram_in(nc, "w_dn1b", [4096, D])
    gmb = _dram_in(nc, "gmb", [128, KC])
    w_cin = _dram_in(nc, "w_cin", [D, 6144])
    x2T = _dram_out(nc, "x2T", [D, TOK])
    bT = _dram_out(nc, "bT", [D, TOK])
    cuT = _dram_out(nc, "cuT", [D, TOK])
    c = TokCtx(nc, T)
    P = c.P
    g_og_s = c.load_vec("g_og_s", g_og, [128, 1])
    g2s = c.load_vec("g2s", g2, [128, KC])
    g1bs = c.load_vec("g1bs", g1b, [128, KC])
    gmbs = c.load_vec("gmbs", gmb, [128, KC])
    _xa_setup(c, xa)
    hr, hk = c.hb_rhs()
    ar, ak = c.ab_rhs()
    for ti in range(TOK // T):
        t0 = ti * T
        c.load_x(x1T, t0)
        for h in range(4):
            P.op("pool", lambda e, h=h, t0=t0: e.dma_start(out=c.ab[:, h, :], in_=osbT[h * 128:(h + 1) * 128, t0:t0 + T]),
                 writes=[("ab", h)], dma=True)
        for h in range(12):
            ta = c.next_tmp()
            tb = c.next_tmp()
            P.op("sp", lambda e, h=h, ta=ta, t0=t0: e.dma_start(out=c.tmp[ta][:], in_=oglT[h * 128:(h + 1) * 128, t0:t0 + T]),
                 writes=[("tmp", ta)], dma=True)
            P.op("sp", lambda e, h=h, tb=tb, t0=t0: e.dma_start(out=c.tmp[tb][:], in_=rT[h * 128:(h + 1) * 128, t0:t0 + T]),
                 writes=[("tmp", tb)], dma=True)
            p2 = c.ssq_ps([c.tmp[ta][:]], lambda i, ta=ta: ("tmp", ta))
            c.rstd_from(p2, 128)
            P.op("act", lambda e, tb=tb: e.activation(c.tmp[tb][:], c.tmp[tb][:], AF.Silu),
                 reads=[("tmp", tb)], writes=[("tmp", tb)])
            P.op("dve", lambda e, ta=ta: e.scalar_tensor_tensor(
                c.tmp[ta][:], c.tmp[ta][:], g_og_s[:, 0:1], c.rstd[:], ALU.mult, ALU.mult),
                reads=[("tmp", ta), "g_og_s", "rstd"], writes=[("tmp", ta)])
            P.op("dve", lambda e, ta=ta, tb=tb, h=h: e.tensor_tensor(
                c.ab[:, 4 + h, :], c.tmp[ta][:], c.tmp[tb][:], ALU.mult),
                reads=[("tmp", ta), ("tmp", tb)], writes=[("ab", 4 + h)])
        c.proj_add(w_out, ar, ak, kc=KC)
        _xa_run(c, xa)
        c.ffn(g2s, "g2s", w_gu2, w_dn2)
        c.ffn(g1bs, "g1bs", w_gu1b, w_dn1b)
        c.store_x(x2T, t0)
        c.norm_x(gmbs, "gmbs")
        for fc in range(KC):
            s = c.load_w(w_cin[:, fc * 128:(fc + 1) * 128])
            p = c.mm(s, hr, hk)
            c.ps_to_dram(p, bT[fc * 128:(fc + 1) * 128, t0:t0 + T], i=fc)
            s = c.load_w(w_cin[:, 2048 + fc * 128:2048 + (fc + 1) * 128])
            pc = c.mm(s, hr, hk)
            s = c.load_w(w_cin[:, 4096 + fc * 128:4096 + (fc + 1) * 128])
            pu = c.mm(s, hr, hk)
            t = c.next_tmp()
            P.op("act", lambda e, pc=pc, t=t: e.copy(c.tmp[t][:], c.pt[pc][:]),
                 reads=[("pt", pc)], writes=[("tmp", t)])
            P.op("dve", lambda e, pu=pu, t=t: e.tensor_tensor(c.tmp[t][:], c.tmp[t][:], c.pt[pu][:], ALU.mult),
                 reads=[("tmp", t), ("pt", pu)], writes=[("tmp", t)])
            P.op("sp", lambda e, t=t, fc=fc, t0=t0: e.dma_start(out=cuT[fc * 128:(fc + 1) * 128, t0:t0 + T], in_=c.tmp[t][:]),
                 reads=[("tmp", t)], dma=True, is_out=True)
    P.emit()
    return nc


def build_l4(TOK, T):
    nc = bass.Bass("TRN2", target_bir_lowering=False)
    x2T = _dram_in(nc, "x2T", [D, TOK])
    bT = _dram_in(nc, "bT", [D, TOK])
    cuT = _dram_in(nc, "cuT", [D, TOK])
    halo = _dram_in(nc, "halo", [D, 2])
    cw = _dram_in(nc, "cw", [128, KC, 3])
    w_cout = _dram_in(nc, "w_cout", [D, D])
    xa = _xa_inputs(nc, "")
    g2 = _dram_in(nc, "g2", [128, KC])
    w_gu2 = _dram_in(nc, "w_gu2", [D, 8192])
    w_dn2 = _dram_in(nc, "w_dn2", [4096, D])
    outT = _dram_out(nc, "outT", [D, TOK])
    c = TokCtx(nc, T)
    P = c.P
    cws = c.load_vec("cws", cw, [128, KC, 3])
    g2s = c.load_vec("g2s", g2, [128, KC])
    _xa_setup(c, xa)
    cub = [nc.alloc_sbuf_tensor("cub%d" % i, [128, T + 2], F32) for i in range(2)]
    ar, ak = c.ab_rhs()
    for ti in range(TOK // T):
        t0 = ti * T
        c.load_x(x2T, t0)
        for fc in range(KC):
            q = fc % 2
            rows = slice(fc * 128, (fc + 1) * 128)
            if ti == 0:
                P.op("sp", lambda e, q=q, rows=rows: e.dma_start(out=cub[q][:, 0:2], in_=halo[rows, :]),
                     writes=[("cub", q)], dma=True)
                P.op("sp", lambda e, q=q, rows=rows: e.dma_start(out=cub[q][:, 2:T + 2], in_=cuT[rows, 0:T]),
                     writes=[("cubm", q)], dma=True)
            else:
                P.op("sp", lambda e, q=q, rows=rows, t0=t0: e.dma_start(out=cub[q][:], in_=cuT[rows, t0 - 2:t0 + T]),
                     writes=[("cub", q), ("cubm", q)], dma=True)
            tb = c.next_tmp()
            P.op("sp", lambda e, tb=tb, rows=rows, t0=t0: e.dma_start(out=c.tmp[tb][:], in_=bT[rows, t0:t0 + T]),
                 writes=[("tmp", tb)], dma=True)
            ty = c.next_tmp()
            eng = "dve"
            rd = [("cub", q), ("cubm", q), "cws"]
            P.op(eng, lambda e, q=q, ty=ty, fc=fc: e.tensor_scalar_mul(
                c.tmp[ty][:], cub[q][:, 2:T + 2], cws[:, fc, 2:3]),
                reads=rd, writes=[("tmp", ty)])
            P.op(eng, lambda e, q=q, ty=ty, fc=fc: e.scalar_tensor_tensor(
                c.tmp[ty][:], cub[q][:, 1:T + 1], cws[:, fc, 1:2], c.tmp[ty][:], ALU.mult, ALU.add),
                reads=rd + [("tmp", ty)], writes=[("tmp", ty)])
            P.op(eng, lambda e, q=q, ty=ty, fc=fc: e.scalar_tensor_tensor(
                c.tmp[ty][:], cub[q][:, 0:T], cws[:, fc, 0:1], c.tmp[ty][:], ALU.mult, ALU.add),
                reads=rd + [("tmp", ty)], writes=[("tmp", ty)])
            P.op(eng, lambda e, ty=ty, tb=tb, fc=fc: e.tensor_tensor(
                c.ab[:, fc, :], c.tmp[ty][:], c.tmp[tb][:], ALU.mult),
                reads=[("tmp", ty), ("tmp", tb)], writes=[("ab", fc)])
        c.proj_add(w_cout, ar, ak, kc=KC)
        _xa_run(c, xa)
        c.ffn(g2s, "g2s", w_gu2, w_dn2)
        c.store_x(outT, t0)
    P.emit()
    return nc


_CACHE = {}


def _get(name, fn, *args):
    k = (name,) + args
    if k not in _CACHE:
        _CACHE[k] = fn(*args)
    return _CACHE[k]


def _f32(a):
    return np.ascontiguousarray(np.asarray(a, dtype=np.float32))


def _consts():
    i = np.arange(128)
    Lm = -(i[:, None] >= i[None, :]).astype(np.float32)
    Um = (i[:, None] <= i[None, :]).astype(np.float32)
    U16 = Um / 16.0
    SU16 = (i[:, None] > i[None, :]).astype(np.float32) / 16.0
    msk = np.zeros((4, 128, 512), np.float32)
    for r in range(4):
        for qb in range(4):
            if qb > r:
                msk[r, :, qb * 128:(qb + 1) * 128] = 1.0
            elif qb == r:
                msk[r, :, qb * 128:(qb + 1) * 128] = (i[:, None] < i[None, :])
    return Lm, Um, U16, SU16, msk


def _xa_maps(inp, layer, b):
    return {
        "g_xa": _col(inp["xa_norm"][layer]), "memT": _f32(np.asarray(inp["mem"][b]).T),
        "g_mem": _col(inp["mem_norm"][layer]), "w_q": _f32(inp["xa_w_q"][layer]),
        "w_kv": _f32(inp["xa_w_kv"][layer]),
        "xgq": _f32(np.asarray(inp["xa_q_norm"][layer]).reshape(128, 1)),
        "xgk": _f32(np.asarray(inp["xa_k_norm"][layer]).reshape(128, 1)),
        "w_o": _f32(inp["xa_w_o"][layer]),
    }


def kernel_unfused(**inp):
    inp = {k: np.asarray(v) for k, v in inp.items()}
    x = inp["x"]
    B, S, _ = x.shape
    CPB = NCORES // B
    TOK = S // CPB
    T = min(1024, TOK)
    cores = list(range(NCORES))

    nc1 = _get("l1", build_l1, TOK, T)
    shared1 = {
        "g1": _col(inp["ffn1_norm"][0]), "w_gu": _f32(inp["ffn1_w_gu"][0]), "w_dn": _f32(inp["ffn1_w_down"][0]),
        "gm": _col(inp["mix_norm"][0]), "w_in": _f32(inp["ab_w_in"][0]),
        "gq": _f32(inp["sb_q_norm"][0].reshape(128, 1)), "gk": _f32(inp["sb_k_norm"][0].reshape(128, 1)),
    }
    maps = []
    for c in cores:
        b, j = divmod(c, CPB)
        m = dict(shared1)
        m["xT"] = _f32(x[b, j * TOK:(j + 1) * TOK, :].T)
        maps.append(m)
    r1 = run_bass_kernel_spmd(nc1, maps, core_ids=cores).results
    x1T = [r["x1T"] for r in r1]
    projT = [np.concatenate([r1[b * CPB + j]["projT"] for j in range(CPB)], axis=1) for b in range(B)]

    nc2 = _get("l2", build_l2, S)
    Lm, Um, U16, SU16, msk = _consts()
    maps = []
    for c in cores:
        b, j = divmod(c, CPB)
        pj = projT[b]
        hs = [3 * j + hh for hh in range(3)]
        wga = np.zeros((32, 192), np.float32)
        for hh, h in enumerate(hs):
            wga[0:16, hh * 64:(hh + 1) * 64] = inp["gla_w_gate"][0][:, h * 64:(h + 1) * 64]
            wga[16, hh * 64:(hh + 1) * 64] = inp["gla_b_gate"][0][h * 64:(h + 1) * 64]
        gqT = np.stack([pj[1536 + h * 64:1536 + (h + 1) * 64] for h in hs])
        gkT = np.stack([pj[2304 + h * 64:2304 + (h + 1) * 64] for h in hs])
        gvT = np.stack([pj[3072 + h * 128:3072 + (h + 1) * 128] for h in hs])
        maps.append({
            "qn": _f32(pj[j * 128:(j + 1) * 128]), "kn": _f32(pj[512 + j * 128:512 + (j + 1) * 128]),
            "vs": _f32(pj[1024 + j * 128:1024 + (j + 1) * 128].T),
            "Lm": Lm, "Um": Um, "U16": U16, "SU16": SU16, "msk": msk,
            "gqT": _f32(gqT), "gkT": _f32(gkT), "gkt": _f32(gkT.transpose(0, 2, 1)),
            "gvt": _f32(gvT.transpose(0, 2, 1)), "glr": _f32(pj[6144:6160]), "wga": wga,
        })
    r2 = run_bass_kernel_spmd(nc2, maps, core_ids=cores).results
    osbT = [np.concatenate([r2[b * CPB + j]["osb"] for j in range(CPB)], axis=0) for b in range(B)]
    oglT = [np.concatenate([r2[b * CPB + j]["ogl"].reshape(3 * 128, S) for j in range(CPB)], axis=0)
            for b in range(B)]

    nc3 = _get("l3", build_l3, TOK, T)
    maps = []
    for c in cores:
        b, j = divmod(c, CPB)
        sl = slice(j * TOK, (j + 1) * TOK)
        m = {
            "x1T": x1T[c], "osbT": _f32(osbT[b][:, sl]), "oglT": _f32(oglT[b][:, sl]),
            "rT": _f32(r1[c]["projT"][4608:6144]),
            "g_og": _f32(inp["gla_o_norm"][0].reshape(128, 1)), "w_out": _f32(inp["ab_w_out"][0]),
            "g2": _col(inp["ffn2_norm"][0]), "w_gu2": _f32(inp["ffn2_w_gu"][0]), "w_dn2": _f32(inp["ffn2_w_down"][0]),
            "g1b": _col(inp["ffn1_norm"][1]), "w_gu1b": _f32(inp["ffn1_w_gu"][1]),
            "w_dn1b": _f32(inp["ffn1_w_down"][1]), "gmb": _col(inp["mix_norm"][1]),
            "w_cin": _f32(inp["conv_w_in"][0]),
        }
        m.update(_xa_maps(inp, 0, b))
        maps.append(m)
    r3 = run_bass_kernel_spmd(nc3, maps, core_ids=cores).results

    nc4 = _get("l4", build_l4, TOK, T)
    cw = _f32(np.asarray(inp["conv_w"][0]).reshape(3, KC, 128).transpose(2, 1, 0))
    maps = []
    for c in cores:
        b, j = divmod(c, CPB)
        halo = np.zeros((D, 2), np.float32) if j == 0 else _f32(r3[c - 1]["cuT"][:, -2:])
        m = {
            "x2T": r3[c]["x2T"], "bT": r3[c]["bT"], "cuT": r3[c]["cuT"], "halo": halo, "cw": cw,
            "w_cout": _f32(inp["conv_w_out"][0]),
            "g2": _col(inp["ffn2_norm"][1]), "w_gu2": _f32(inp["ffn2_w_gu"][1]), "w_dn2": _f32(inp["ffn2_w_down"][1]),
        }
        m.update(_xa_maps(inp, 1, b))
        maps.append(m)
    r4 = run_bass_kernel_spmd(nc4, maps, core_ids=cores).results
    out = np.empty((B, S, D), np.float32)
    for c in cores:
        b, j = divmod(c, CPB)
        out[b, j * TOK:(j + 1) * TOK, :] = r4[c]["outT"].T
    return out


SBUF_LO = 16512
SBUF_HI = 229344
RG8 = [list(range(NCORES))]
GH = 768
GF = 400
GB = 512


class Arena:
    def __init__(self, nc, base, limit, prefix):
        self.nc, self.cur, self.limit, self.prefix = nc, base, limit, prefix

    def __call__(self, name, shape, dt):
        n = 1
        for d in shape[1:]:
            n *= d
        nbytes = (n * (4 if dt == F32 else 2) + 63) // 64 * 64
        off = self.cur
        self.cur += nbytes
        assert self.cur <= self.limit, (self.prefix, name, self.cur, self.limit)
        return self.nc.alloc_sbuf_tensor_at(self.prefix + name, list(shape), dt, offset=off)


def _pidbj(e):
    pid = e.partition_id()
    return pid // 4, pid % 4


_DYN = {}


def _didx(e, jx):
    pid = e.partition_id()
    if jx == 4:
        return (pid + (NCORES - 1)) % NCORES
    return (pid // 4) * 16 + (pid % 4) + jx * 4


def build_fused(S):
    _DYN.clear()
    TOK = S // 4
    T = min(1024, TOK)
    NTL = TOK // T
    nc = bass.Bass("TRN2", target_bir_lowering=False)
    din = lambda n, sh, dt=F32: _dram_in(nc, n, sh, dt)
    dint = lambda n, sh, dt=F32: nc.dram_tensor(n, list(sh), dt)
    xT = din("xT", [D, TOK])
    L = []
    for l in range(2):
        sf = str(l)
        d = dict(g1=din("g1_" + sf, [128, KC]), w_gu1=din("w_gu1_" + sf, [D, 8192]), w_dn1=din("w_dn1_" + sf, [4096, D]),
                 gm=din("gm_" + sf, [128, KC]),
                 g2=din("g2_" + sf, [128, KC]), w_gu2=din("w_gu2_" + sf, [D, 8192]), w_dn2=din("w_dn2_" + sf, [4096, D]),
                 xa=_xa_inputs(nc, "_" + sf))
        L.append(d)
    w_in = din("w_in", [D, AB_W])
    gq = din("gq", [128, 1])
    gk = din("gk", [128, 1])
    g_og = din("g_og", [128, 1])
    w_out = din("w_out", [D, D])
    wga4 = din("wga", [32, 192])
    w_cin = din("w_cin", [D, 6144])
    cw = din("cw", [128, KC, 3])
    w_cout = din("w_cout", [D, D])
    Lm = din("Lm", [128, 128])
    Um = din("Um", [128, 128])
    U16 = din("U16", [128, 128])
    SU16 = din("SU16", [128, 128])
    msk = din("msk", [4, 128, 512])
    ident = din("ident", [128, 128])
    hsel = din("hsel", [128, 1])
    outT = _dram_out(nc, "outT", [D, TOK])
    x1T = dint("x1T", [D, TOK]).ap()
    rT = dint("rT", [1536, TOK]).ap()
    x2T = dint("x2T", [D, TOK]).ap()
    bT = dint("bT", [D, TOK]).ap()
    cuT = dint("cuT", [D, TOK]).ap()
    exAh_in = dint("exAh_in", [4 * GH, TOK], BF16)
    exAf_in = dint("exAf_in", [4 * GF, TOK])
    exAh = dint("exAh", [NCORES * 4 * GH, TOK], BF16)
    exAf = dint("exAf", [NCORES * 4 * GF, TOK])
    exB1_in = dint("exB1_in", [4 * 128, TOK], BF16)
    exB2_in = dint("exB2_in", [4 * 384, TOK])
    exB1 = dint("exB1", [NCORES * 4 * 128, TOK], BF16)
    exB2 = dint("exB2", [NCORES * 4 * 384, TOK])
    locAh = dint("locAh", [4 * GH, TOK], BF16).ap()
    locAf = dint("locAf", [4 * GF, TOK]).ap()
    locB1 = dint("locB1", [4 * 128, TOK], BF16).ap()
    locB2 = dint("locB2", [4 * 384, TOK]).ap()
    hal_in = dint("hal_in", [D, 2])
    hal = dint("hal", [NCORES * D, 2])

    P = Prog(nc)
    pers = Arena(nc, SBUF_LO, SBUF_LO + 20 * 1024, "p_")
    ARENA_LO = SBUF_LO + 20 * 1024
    a1 = Arena(nc, ARENA_LO, SBUF_HI, "t_")
    a2 = Arena(nc, ARENA_LO, SBUF_HI, "m_")
    NPT = 4 if T > 512 else 8
    pt = [nc.alloc_psum_tensor("pt%d" % i, [128, T], F32) for i in range(NPT)]
    if T > 512:
        banks = [pt[i // 2][:, (i % 2) * 512:(i % 2 + 1) * 512] for i in range(8)]
    else:
        banks = [pt[i][:, :] for i in range(8)]
    c = TokCtx(nc, T, P=P, alloc=a1, palloc=pers, pt=pt)
    hr, hk = c.hb_rhs()
    ar, ak = c.ab_rhs()

    g1s = c.load_vec("g1s", L[0]["g1"], [128, KC])
    gms = c.load_vec("gms", L[0]["gm"], [128, KC])
    gqs = c.load_vec("gqs", gq, [128, 1])
    gks = c.load_vec("gks", gk, [128, 1])
    P.op("dve", lambda e: e.tensor_scalar_mul(gqs[:], gqs[:], 128.0 ** -0.5), reads=["gqs"], writes=["gqs"])

    def out_bf16(p, dst, m0=0, m=128):
        q = c.next_sq()
        sq = c.sqb[q]
        P.op("act", lambda e: e.copy(sq[m0:m0 + m, :], c.pt[p][m0:m0 + m, :]), reads=[("pt", p)], writes=[("sq", q)])
        P.op("sp", lambda e: e.dma_start(out=dst, in_=sq[m0:m0 + m, :]), reads=[("sq", q)], dma=True)

    def out_f32(p, dsts, i=0):
        t = c.next_tmp()
        m_all = max(m0 + m for m0, m, _ in dsts)
        if i % 2 == 0:
            P.op("act", lambda e: e.copy(c.tmp[t][0:m_all, :], c.pt[p][0:m_all, :]),
                 reads=[("pt", p)], writes=[("tmp", t)])
        else:
            P.op("dve", lambda e: e.tensor_copy(c.tmp[t][0:m_all, :], c.pt[p][0:m_all, :]),
                 reads=[("pt", p)], writes=[("tmp", t)])
        for m0, m, dst in dsts:
            P.op("sp", lambda e, m0=m0, m=m, dst=dst: e.dma_start(out=dst, in_=c.tmp[t][m0:m0 + m, :]),
                 reads=[("tmp", t)], dma=True)

    Ah = exAh_in.ap()
    Af = exAf_in.ap()
    for ti in range(NTL):
        t0 = ti * T
        ts_ = slice(t0, t0 + T)
        c.load_x(xT, t0)
        c.ffn(g1s, "g1s", L[0]["w_gu1"], L[0]["w_dn1"])
        for k in range(KC):
            P.op("sp", lambda e, k=k, ts_=ts_: e.dma_start(out=x1T[k * 128:(k + 1) * 128, ts_], in_=c.xs[:, k, :]),
                 reads=[("xs", k)], dma=True)
        c.norm_x(gms, "gms")
        for oc in range(48):
            s = c.load_w(w_in[:, oc * 128:(oc + 1) * 128])
            p = c.mm(s, hr, hk)
            if oc < 8:
                hh = oc % 4
                p2 = c.ssq_ps([c.pt[p][:]], lambda i, p=p: ("pt", p))
                c.rstd_from(p2, 128)
                q = c.next_sq()
                gs, gn = (gqs, "gqs") if oc < 4 else (gks, "gks")
                P.op("dve", lambda e, p=p, q=q, gs=gs: e.scalar_tensor_tensor(
                    c.sqb[q][:], c.pt[p][:], gs[:, 0:1], c.rstd[:], ALU.mult, ALU.mult),
                    reads=[("pt", p), gn, "rstd"], writes=[("sq", q)])
                r0 = hh * GH + (0 if oc < 4 else 128)
                P.op("sp", lambda e, q=q, r0=r0, ts_=ts_: e.dma_start(out=Ah[r0:r0 + 128, ts_], in_=c.sqb[q][:]),
                     reads=[("sq", q)], dma=True)
            elif oc < 12:
                hh = oc - 8
                out_bf16(p, Ah[hh * GH + 256:hh * GH + 384, ts_])
            elif oc < 24:
                m = (oc - 12) % 6
                base = 0 if oc < 18 else 192
                dsts = []
                for half in range(2):
                    hd = 2 * m + half
                    r0 = (hd // 3) * GF + base + (hd % 3) * 64
                    dsts.append((half * 64, 64, Af[r0:r0 + 64, ts_]))
                out_f32(p, dsts, i=oc)
            elif oc < 36:
                hd = oc - 24
                r0 = (hd // 3) * GH + 384 + (hd % 3) * 128
                out_bf16(p, Ah[r0:r0 + 128, ts_])
            else:
                hd = oc - 36
                out_f32(p, [(0, 128, rT[hd * 128:(hd + 1) * 128, ts_])], i=oc)
        s = c.load_w(w_in[:, 6144:6160], ncols=16)
        p = c.mm(s, hr, hk, m=16)
        out_f32(p, [(0, 16, Af[g * GF + 384:g * GF + 400, ts_]) for g in range(4)])
    P.barrier()
    P.op("pool", lambda e: e.collective_compute("AllGather", ALU.bypass, replica_groups=RG8,
                                                ins=[exAh_in.ap().opt()], outs=[exAh.ap().opt()]),
         dma=True, cc=True)
    P.op("pool", lambda e: e.collective_compute("AllGather", ALU.bypass, replica_groups=RG8,
                                                ins=[exAf_in.ap().opt()], outs=[exAf.ap().opt()]),
         dma=True, cc=True)
    P.barrier()
    def slab(src, dst, r):
        s5 = src.ap().rearrange("(b p j r) t -> b p j r t", b=2, p=4, j=4, r=r)
        d3 = dst.rearrange("(p r) t -> p r t", r=r)

        def f(e):
            pid = e.partition_id()
            return e.dma_start(out=d3, in_=s5[pid // 4, :, pid % 4, :, :])
        return f
    P.op("sp", slab(exAh, locAh, GH), dma=True)
    P.op("pool", slab(exAf, locAf, GF), dma=True)
    P.barrier()

    _phase2(nc, P, a2, banks, S, TOK, locAh, locAf, exB1_in, exB2_in, wga4, Lm, Um, U16, SU16, msk, ident)
    P.barrier()
    P.op("pool", lambda e: e.collective_compute("AllGather", ALU.bypass, replica_groups=RG8,
                                                ins=[exB1_in.ap().opt()], outs=[exB1.ap().opt()]),
         dma=True, cc=True)
    P.op("pool", lambda e: e.collective_compute("AllGather", ALU.bypass, replica_groups=RG8,
                                                ins=[exB2_in.ap().opt()], outs=[exB2.ap().opt()]),
         dma=True, cc=True)
    P.barrier()
    P.op("sp", slab(exB1, locB1, 128), dma=True)
    P.op("pool", slab(exB2, locB2, 384), dma=True)
    P.barrier()

    P.op("dve", lambda e: e.memset(c.ones[:], 1.0), writes=["ones"])
    P.op("dve", lambda e: e.memset(c.ones_d[:], 1.0 / D), writes=["ones"])
    P.op("dve", lambda e: e.memset(c.ones_h[:], 1.0 / 128), writes=["ones"])
    P.op("dve", lambda e: e.memset(c.eps_c[:], EPS), writes=["eps_c"])
    g_og_s = c.load_vec("g_og_s", g_og, [128, 1])
    g2s0 = c.load_vec("g2s0", L[0]["g2"], [128, KC])
    g1s1 = c.load_vec("g1s1", L[1]["g1"], [128, KC])
    gms1 = c.load_vec("gms1", L[1]["gm"], [128, KC])
    g2s1 = c.load_vec("g2s1", L[1]["g2"], [128, KC])
    cws = c.load_vec("cws", cw, [128, KC, 3])
    hsel_s = c.load_vec("hsel_s", hsel, [128, 1])
    _xa_setup(c, L[0]["xa"], sfx="_0")
    _xa_setup(c, L[1]["xa"], sfx="_1")
    def exb_rows(e, jg, off, n, ts_):
        return locB2[jg * 384 + off - 128:jg * 384 + off - 128 + n, ts_]

    for ti in range(NTL):
        t0 = ti * T
        ts_ = slice(t0, t0 + T)
        c.load_x(x1T, t0)
        for jg in range(4):
            P.op("sp", lambda e, jg=jg, ts_=ts_: e.dma_start(out=c.ab[:, jg, :], in_=locB1[jg * 128:(jg + 1) * 128, ts_]),
                 writes=[("ab", jg)], dma=True)
        for h in range(12):
            jg, hh = divmod(h, 3)
            ta = c.next_tmp()
            tb = c.next_tmp()
            P.op("sp", lambda e, jg=jg, hh=hh, ta=ta, ts_=ts_: e.dma_start(
                out=c.tmp[ta][:], in_=exb_rows(e, jg, 128 + hh * 128, 128, ts_)), writes=[("tmp", ta)], dma=True)
            P.op("sp", lambda e, h=h, tb=tb, ts_=ts_: e.dma_start(out=c.tmp[tb][:], in_=rT[h * 128:(h + 1) * 128, ts_]),
                 writes=[("tmp", tb)], dma=True)
            p2 = c.ssq_ps([c.tmp[ta][:]], lambda i, ta=ta: ("tmp", ta))
            c.rstd_from(p2, 128)
            P.op("act", lambda e, tb=tb: e.activation(c.tmp[tb][:], c.tmp[tb][:], AF.Silu),
                 reads=[("tmp", tb)], writes=[("tmp", tb)])
            P.op("dve", lambda e, ta=ta: e.scalar_tensor_tensor(
                c.tmp[ta][:], c.tmp[ta][:], g_og_s[:, 0:1], c.rstd[:], ALU.mult, ALU.mult),
                reads=[("tmp", ta), "g_og_s", "rstd"], writes=[("tmp", ta)])
            P.op("dve", lambda e, ta=ta, tb=tb, h=h: e.tensor_tensor(
                c.ab[:, 4 + h, :], c.tmp[ta][:], c.tmp[tb][:], ALU.mult),
                reads=[("tmp", ta), ("tmp", tb)], writes=[("ab", 4 + h)])
        c.proj_add(w_out, ar, ak, kc=KC)
        _xa_run(c, L[0]["xa"])
        c.ffn(g2s0, "g2s0", L[0]["w_gu2"], L[0]["w_dn2"])
        c.ffn(g1s1, "g1s1", L[1]["w_gu1"], L[1]["w_dn1"])
        for k in range(KC):
            P.op("sp", lambda e, k=k, ts_=ts_: e.dma_start(out=x2T[k * 128:(k + 1) * 128, ts_], in_=c.xs[:, k, :]),
                 reads=[("xs", k)], dma=True)
        c.norm_x(gms1, "gms1")
        for fc in range(KC):
            rows = slice(fc * 128, (fc + 1) * 128)
            s = c.load_w(w_cin[:, fc * 128:(fc + 1) * 128])
            p = c.mm(s, hr, hk)
            out_f32(p, [(0, 128, bT[rows, ts_])], i=fc)
            s = c.load_w(w_cin[:, 2048 + fc * 128:2048 + (fc + 1) * 128])
            pc = c.mm(s, hr, hk)
            s = c.load_w(w_cin[:, 4096 + fc * 128:4096 + (fc + 1) * 128])
            pu = c.mm(s, hr, hk)
            t = c.next_tmp()
            P.op("act", lambda e, pc=pc, t=t: e.copy(c.tmp[t][:], c.pt[pc][:]),
                 reads=[("pt", pc)], writes=[("tmp", t)])
            P.op("dve", lambda e, pu=pu, t=t: e.tensor_tensor(c.tmp[t][:], c.tmp[t][:], c.pt[pu][:], ALU.mult),
                 reads=[("tmp", t), ("pt", pu)], writes=[("tmp", t)])
            P.op("sp", lambda e, t=t, rows=rows, ts_=ts_: e.dma_start(out=cuT[rows, ts_], in_=c.tmp[t][:]),
                 reads=[("tmp", t)], dma=True)
            if ti == NTL - 1:
                P.op("sp", lambda e, t=t, rows=rows: e.dma_start(out=hal_in.ap()[rows, :], in_=c.tmp[t][:, T - 2:T]),
                     reads=[("tmp", t)], dma=True)
    P.barrier()
    P.op("pool", lambda e: e.collective_compute("AllGather", ALU.bypass, replica_groups=RG8,
                                                ins=[hal_in.ap().opt()], outs=[hal.ap().opt()]),
         dma=True, cc=True)
    P.barrier()
    hal_loc = dint("hal_loc", [D, 2]).ap()
    HL3 = hal.ap().rearrange("(g r) t -> g r t", r=D)
    P.op("sp", lambda e: e.dma_start(out=hal_loc, in_=HL3[_didx(e, 4), :, :]), dma=True)
    P.barrier()

    cub = [a1("cub%d" % i, [128, T + 2], F32) for i in range(2)]
    hst = [a1("hst%d" % i, [128, 2], F32) for i in range(2)]
    for ti in range(NTL):
        t0 = ti * T
        ts_ = slice(t0, t0 + T)
        c.load_x(x2T, t0)
        for fc in range(KC):
            q = fc % 2
            rows = slice(fc * 128, (fc + 1) * 128)
            if ti == 0:
                P.op("sp", lambda e, q=q, rows=rows: e.dma_start(out=hst[q][:], in_=hal_loc[rows, :]),
                     writes=[("hst", q)], dma=True)
                P.op("dve", lambda e, q=q: e.tensor_scalar_mul(cub[q][:, 0:2], hst[q][:], hsel_s[:, 0:1]),
                     reads=[("hst", q), "hsel_s"], writes=[("cub", q)])
                P.op("sp", lambda e, q=q, rows=rows: e.dma_start(out=cub[q][:, 2:T + 2], in_=cuT[rows, 0:T]),
                     writes=[("cubm", q)], dma=True)
            else:
                P.op("sp", lambda e, q=q, rows=rows, t0=t0: e.dma_start(out=cub[q][:], in_=cuT[rows, t0 - 2:t0 + T]),
                     writes=[("cub", q), ("cubm", q)], dma=True)
            tb = c.next_tmp()
            P.op("sp", lambda e, tb=tb, rows=rows, ts_=ts_: e.dma_start(out=c.tmp[tb][:], in_=bT[rows, ts_]),
                 writes=[("tmp", tb)], dma=True)
            ty = c.next_tmp()
            rd = [("cub", q), ("cubm", q), "cws"]
            P.op("dve", lambda e, q=q, ty=ty, fc=fc: e.tensor_scalar_mul(
                c.tmp[ty][:], cub[q][:, 2:T + 2], cws[:, fc, 2:3]), reads=rd, writes=[("tmp", ty)])
            P.op("dve", lambda e, q=q, ty=ty, fc=fc: e.scalar_tensor_tensor(
                c.tmp[ty][:], cub[q][:, 1:T + 1], cws[:, fc, 1:2], c.tmp[ty][:], ALU.mult, ALU.add),
                reads=rd + [("tmp", ty)], writes=[("tmp", ty)])
            P.op("dve", lambda e, q=q, ty=ty, fc=fc: e.scalar_tensor_tensor(
                c.tmp[ty][:], cub[q][:, 0:T], cws[:, fc, 0:1], c.tmp[ty][:], ALU.mult, ALU.add),
                reads=rd + [("tmp", ty)], writes=[("tmp", ty)])
            P.op("dve", lambda e, ty=ty, tb=tb, fc=fc: e.tensor_tensor(
                c.ab[:, fc, :], c.tmp[ty][:], c.tmp[tb][:], ALU.mult),
                reads=[("tmp", ty), ("tmp", tb)], writes=[("ab", fc)])
        c.proj_add(w_cout, ar, ak, kc=KC)
        _xa_run(c, L[1]["xa"])
        c.ffn(g2s1, "g2s1", L[1]["w_gu2"], L[1]["w_dn2"])
        c.store_x(outT, t0)
    P.emit()
    return nc


def _phase2(nc, P, a, banks, S, TOK, exAh, exAf, exB1_in, exB2_in, wga4, Lm, Um, U16, SU16, msk, ident):
    NB = S // 128
    NG = S // 512
    Bo1 = exB1_in.ap()
    Bo2 = exB2_in.ap()

    def rows_h(e, jpp, off, n, cs):
        return exAh[jpp * GH + off:jpp * GH + off + n, cs]

    def rows_f(e, jpp, off, n, cs):
        return exAf[jpp * GF + off:jpp * GF + off + n, cs]

    qs = a("qs", [128, S], BF16)
    ks = a("ks", [128, S], BF16)
    vsb = a("vsb", [128, S], BF16)
    Ls = a("Ls", [128, 128], BF16)
    ones = a("ones", [128, 128], BF16)
    idb = a("idb", [128, 128], BF16)
    idf = a("idf", [128, 128], F32)
    mk = a("mk", [128, 4, 512], F32)
    one_c = a("one_c", [128, 1], F32)
    CH = min(2048, TOK)
    vst = [a("vst%d" % i, [128, CH], BF16) for i in range(2)]
    P.op("pool", lambda e: e.dma_start(out=Ls[:], in_=Lm), writes=["Ls"], dma=True)
    P.op("pool", lambda e: e.dma_start(out=idb[:], in_=ident), writes=["idb"], dma=True)
    P.op("sp", lambda e: e.dma_start(out=idf[:], in_=ident), writes=["idf"], dma=True)
    P.op("sp", lambda e: e.dma_start(out=mk[:], in_=msk.rearrange("r p t -> p r t")), writes=["mk"], dma=True)
    P.op("dve", lambda e: e.memset(ones[:], 1.0), writes=["ones"])
    P.op("dve", lambda e: e.memset(one_c[:], 1.0), writes=["one_c"])
    zp = banks[0:3]
    cp = banks[3:5]
    op_ = banks[5:7]
    gp = banks[7]
    for i in range(S // CH):
        s0 = i * CH
        jpp, c0 = divmod(s0, TOK)
        cs = slice(c0, c0 + CH)
        P.op("sp", lambda e, i=i, jpp=jpp, cs=cs: e.dma_start(out=qs[:, i * CH:(i + 1) * CH], in_=rows_h(e, jpp, 0, 128, cs)),
             writes=[("qs", i)], dma=True)
        P.op("sp", lambda e, i=i, jpp=jpp, cs=cs: e.dma_start(out=ks[:, i * CH:(i + 1) * CH], in_=rows_h(e, jpp, 128, 128, cs)),
             writes=[("ks", i)], dma=True)
        v = i % 2
        P.op("sp", lambda e, v=v, jpp=jpp, cs=cs: e.dma_start(out=vst[v][:], in_=rows_h(e, jpp, 256, 128, cs)),
             writes=[("vst", v)], dma=True)
        for g4 in range(CH // 512):
            bk_ = (i * (CH // 512) + g4) % 2
            for n in range(4):
                P.op("pe", lambda e, v=v, g4=g4, n=n, bk_=bk_: e.matmul(
                    cp[bk_][:, n * 128:(n + 1) * 128], vst[v][:, g4 * 512 + n * 128:g4 * 512 + (n + 1) * 128], idb[:],
                    start=True, stop=True),
                    reads=[("vst", v), "idb"], writes=[("cp", bk_)])
            P.op("act", lambda e, i=i, g4=g4, bk_=bk_: e.copy(
                vsb[:, i * CH + g4 * 512:i * CH + (g4 + 1) * 512], cp[bk_][:]),
                reads=[("cp", bk_)], writes=[("vsb", i)])
    e1 = [a("e1_%d" % i, [128, 512], F32) for i in range(2)]
    spb = [a("spb%d" % i, [128, 512], BF16) for i in range(3)]
    tl = [a("tl%d" % i, [128, 512], F32) for i in range(2)]
    wbf = [a("wbf%d" % i, [128, 512], BF16) for i in range(3)]
    Rb = [a("Rb%d" % i, [128, 512], F32) for i in range(2)]
    ost = [a("ost%d" % i, [128, 512], BF16) for i in range(2)]
    NZ = 3

    steps = []
    for G in range(NG):
        for n in range(4 * G + 3, -1, -1):
            steps.append((G, n))
    st = {}

    def stage_a(i):
        G, n = steps[i]
        zi = i % NZ
        ei = i % 2
        si = i % 3
        st[i] = (zi, ei, si)
        qc = (G * 512) // CH
        kc_ = (n * 128) // CH
        P.op("pe", lambda e: e.matmul(zp[zi][:], ks[:, n * 128:(n + 1) * 128], qs[:, G * 512:(G + 1) * 512],
                                      start=True, stop=True),
             reads=[("ks", kc_), ("qs", qc)], writes=[("zp", zi)])
        P.op("act", lambda e: e.activation(e1[ei][:], zp[zi][:], AF.Exp),
             reads=[("zp", zi)], writes=[("e1", ei)])
        r = n - 4 * G
        if r >= 0:
            P.op("act", lambda e: e.activation(e1[ei][:], e1[ei][:], AF.Ln, bias=one_c[:]),
                 reads=[("e1", ei), "one_c"], writes=[("e1", ei)])
            P.op("pool", lambda e: e.tensor_tensor(spb[si][:], e1[ei][:], mk[:, r, :], ALU.mult),
                 reads=[("e1", ei), "mk"], writes=[("spb", si)])
        else:
            P.op("act", lambda e: e.activation(spb[si][:], e1[ei][:], AF.Ln, bias=one_c[:]),
                 reads=[("e1", ei), "one_c"], writes=[("spb", si)])

    def stage_b(i):
        G, n = steps[i]
        zi, ei, si = st[i]
        ci = i % 2
        ti = i % 2
        wi = i % 3
        rb = G % 2
        first = (n == 4 * G + 3)
        P.op("pe", lambda e: e.matmul(zp[zi][:], Ls[:], spb[si][:], start=False, stop=True),
             reads=["Ls", ("spb", si)], writes=[("zp", zi)])
        P.op("pe", lambda e: e.matmul(cp[ci][:], ones[:], spb[si][:], start=True, stop=True),
             reads=["ones", ("spb", si)], writes=[("cp", ci)])
        if first:
            P.op("dve", lambda e: e.memset(Rb[rb][:], 0.0), writes=[("Rb", rb)])
        P.op("dve", lambda e: e.tensor_tensor(tl[ti][:], zp[zi][:], Rb[rb][:], ALU.subtract),
             reads=[("zp", zi), ("Rb", rb)], writes=[("tl", ti)])
        r = n - 4 * G
        if r >= 0:
            P.op("act", lambda e: e.activation(tl[ti][:], tl[ti][:], AF.Exp),
                 reads=[("tl", ti)], writes=[("tl", ti)])
            P.op("pool", lambda e: e.tensor_tensor(wbf[wi][:], tl[ti][:], mk[:, r, :], ALU.mult),
                 reads=[("tl", ti), "mk"], writes=[("wbf", wi)])
        else:
            P.op("act", lambda e: e.activation(wbf[wi][:], tl[ti][:], AF.Exp),
                 reads=[("tl", ti)], writes=[("wbf", wi)])
        P.op("dve", lambda e: e.tensor_tensor(Rb[rb][:], Rb[rb][:], cp[ci][:], ALU.add),
             reads=[("Rb", rb), ("cp", ci)], writes=[("Rb", rb)])

    def stage_c(i):
        G, n = steps[i]
        wi = i % 3
        oi = G % 2
        vc = (n * 128) // CH
        P.op("pe", lambda e: e.matmul(op_[oi][:], vsb[:, n * 128:(n + 1) * 128], wbf[wi][:],
                                      start=(n == 4 * G + 3), stop=(n == 0)),
             reads=[("vsb", vc), ("wbf", wi)], writes=[("op", oi)])
        if n == 0:
            s0 = G * 512
            ch, c0 = divmod(s0, TOK)
            P.op("act", lambda e: e.copy(ost[oi][:], op_[oi][:]), reads=[("op", oi)], writes=[("ost", oi)])
            P.op("sp", lambda e: e.dma_start(out=Bo1[ch * 128:(ch + 1) * 128, c0:c0 + 512], in_=ost[oi][:]),
                 reads=[("ost", oi)], dma=True)

    ns = len(steps)
    for i in range(ns + 2):
        if i < ns:
            stage_a(i)
        if 0 <= i - 1 < ns:
            stage_b(i - 1)
        if 0 <= i - 2 < ns:
            stage_c(i - 2)

    P.barrier()
    Us = a("Us", [128, 128], F32)
    U16s = a("U16s", [128, 128], F32)
    SU16s = a("SU16s", [128, 128], F32)
    wgs = a("wgs", [32, 192], F32)
    P.op("sp", lambda e: e.dma_start(out=Us[:], in_=Um), writes=["Us"], dma=True)
    P.op("sp", lambda e: e.dma_start(out=U16s[:], in_=U16), writes=["U16s"], dma=True)
    P.op("sp", lambda e: e.dma_start(out=SU16s[:], in_=SU16), writes=["SU16s"], dma=True)

    P.op("sp", lambda e: e.dma_start(out=wgs[:], in_=wga4), writes=["wgs"], dma=True)
    GC = min(256, TOK)
    NCH = GC // 128
    glrs = [a("glrs%d" % i, [32, GC], F32) for i in range(2)]
    qts = [a("qts%d" % i, [64, 3, GC], F32) for i in range(2)]
    kts = [a("kts%d" % i, [64, 3, GC], F32) for i in range(2)]
    vts = [a("vts%d" % i, [128, 3, GC], BF16) for i in range(2)]
    ogs = [a("ogs%d" % i, [128, 3, GC], F32) for i in range(2)]
    state = [a("state%d" % h, [64, 128], F32) for h in range(3)]
    state_b = [a("stateb%d" % h, [64, 128], BF16) for h in range(3)]
    NR = 3
    gab = [a("gab%d" % i, [128, 64], F32) for i in range(NR)]
    gex = [a("gex%d" % i, [128, 64], F32) for i in range(NR)]
    gmn = [a("gmn%d" % i, [128, 64], F32) for i in range(NR)]
    ls16 = [a("ls16_%d" % i, [128, 64], F32) for i in range(NR)]
    E1 = [a("E1_%d" % i, [64, 128], F32) for i in range(NR)]
    E2 = [a("E2_%d" % i, [64, 128], F32) for i in range(NR)]
    E3 = [a("E3_%d" % i, [128, 64], F32) for i in range(NR)]
    ebl = [a("ebl%d" % i, [64, 1], F32) for i in range(NR)]
    qtl = [a("qtl%d" % i, [64, 128], BF16) for i in range(NR)]
    ktl = [a("ktl%d" % i, [64, 128], BF16) for i in range(NR)]
    khat = [a("khat%d" % i, [128, 64], BF16) for i in range(NR)]
    vtk = [a("vtk%d" % i, [128, 128], BF16) for i in range(NR)]
    Asb = [a("Asb%d" % i, [128, 128], BF16) for i in range(NR)]
    for h in range(3):
        P.op("dve", lambda e, h=h: e.memset(state[h][:], 0.0), writes=[("state", h)])
        P.op("dve", lambda e, h=h: e.memset(state_b[h][:], 0.0), writes=[("stateb", h)])
    for i in range(2):
        P.op("dve", lambda e, i=i: e.memset(glrs[i][:], 1.0), writes=[("glrs", i)])
    g_ps = [gp[:, r * 64:(r + 1) * 64] for r in range(NR)]
    ct_ps = [gp[:, 192 + r * 64:192 + (r + 1) * 64] for r in range(NR)]
    bt_ps = [cp[0][0:64, r * 128:(r + 1) * 128] for r in range(NR)]
    sc_ps = [cp[1][:, r * 128:(r + 1) * 128] for r in range(NR)]
    o_ps = [zp[0][:, r * 128:(r + 1) * 128] for r in range(NR)]
    kv_ps = [zp[1][0:64, r * 128:(r + 1) * 128] for r in range(NR)]
    vt_ps = [zp[2][:, r * 128:(r + 1) * 128] for r in range(NR)]
    kt_ps = [op_[0][:, r * 64:(r + 1) * 64] for r in range(NR)]
    items = []
    for g in range(S // GC):
        for c in range(NCH):
            for h in range(3):
                items.append((g, c, h))
    NI = len(items)

    def loads(g):
        b = g % 2
        t0 = g * GC
        jpp, c0 = divmod(t0, TOK)
        cs = slice(c0, c0 + GC)
        P.op("sp", lambda e: e.dma_start(out=glrs[b][0:16, :], in_=rows_f(e, jpp, 384, 16, cs)),
             writes=[("glrs", b)], dma=True)
        for h in range(3):
            P.op("sp", lambda e, h=h: e.dma_start(out=qts[b][:, h, :], in_=rows_f(e, jpp, h * 64, 64, cs)),
                 writes=[("qts", b, h)], dma=True)
            P.op("sp", lambda e, h=h: e.dma_start(out=kts[b][:, h, :], in_=rows_f(e, jpp, 192 + h * 64, 64, cs)),
                 writes=[("kts", b, h)], dma=True)
            P.op("sp", lambda e, h=h: e.dma_start(out=vts[b][:, h, :], in_=rows_h(e, jpp, 384 + h * 128, 128, cs)),
                 writes=[("vts", b, h)], dma=True)

    def f1(i):
        g, c, h = items[i]
        if c == 0 and h == 0:
            loads(g)
        b = g % 2
        r = i % NR
        cs = slice(c * 128, (c + 1) * 128)
        rk = lambda nm: (nm, r)
        P.op("pe", lambda e: e.matmul(g_ps[r], glrs[b][:, cs], wgs[:, h * 64:(h + 1) * 64], start=True, stop=True),
             reads=[("glrs", b), "wgs"], writes=[rk("g_ps")])
        P.op("dve", lambda e: e.tensor_scalar_min(gmn[r][:], g_ps[r], 0.0),
             reads=[rk("g_ps")], writes=[rk("gmn")])
        P.op("dve", lambda e: e.scalar_tensor_tensor(gab[r][:], g_ps[r], 0.0, gmn[r][:], ALU.max, ALU.subtract),
             reads=[rk("g_ps"), rk("gmn")], writes=[rk("gab")])
        P.op("act", lambda e: e.activation(gex[r][:], gab[r][:], AF.Exp, scale=-1.0),
             reads=[rk("gab")], writes=[rk("gex")])
        P.op("act", lambda e: e.activation(gex[r][:], gex[r][:], AF.Ln, bias=one_c[:]),
             reads=[rk("gex"), "one_c"], writes=[rk("gex")])
        P.op("dve", lambda e: e.tensor_tensor(ls16[r][:], gmn[r][:], gex[r][:], ALU.subtract),
             reads=[rk("gmn"), rk("gex")], writes=[rk("ls16")])
        P.op("pe", lambda e: e.matmul(vt_ps[r], vts[b][:, h, cs], idb[:], start=True, stop=True),
             reads=[("vts", b, h), "idb"], writes=[rk("vt_ps")])
        P.op("act", lambda e: e.copy(vtk[r][:], vt_ps[r]), reads=[rk("vt_ps")], writes=[rk("vtk")])
        P.op("pe", lambda e: e.matmul(kt_ps[r], kts[b][:, h, cs], idf[0:64, 0:64], start=True, stop=True),
             reads=[("kts", b, h), "idf"], writes=[rk("kt_ps")])

    def f2(i):
        g, c, h = items[i]
        b = g % 2
        r = i % NR
        cs = slice(c * 128, (c + 1) * 128)
        rk = lambda nm: (nm, r)
        P.op("pe", lambda e: e.matmul(bt_ps[r], ls16[r][:], U16s[:], start=True, stop=True),
             reads=[rk("ls16"), "U16s"], writes=[rk("bt_ps")])
        P.op("pe", lambda e: e.matmul(ct_ps[r], SU16s[:], ls16[r][:], start=True, stop=True),
             reads=[rk("ls16"), "SU16s"], writes=[rk("ct_ps")])
        P.op("act", lambda e: e.activation(E1[r][:], bt_ps[r], AF.Exp),
             reads=[rk("bt_ps")], writes=[rk("E1")])
        P.op("act", lambda e: e.activation(E2[r][:], bt_ps[r], AF.Exp, scale=-1.0),
             reads=[rk("bt_ps")], writes=[rk("E2")])
        P.op("act", lambda e: e.activation(E3[r][:], ct_ps[r], AF.Exp),
             reads=[rk("ct_ps")], writes=[rk("E3")])
        P.op("act", lambda e: e.activation(ebl[r][:], bt_ps[r][:, 127:128], AF.Exp),
             reads=[rk("bt_ps")], writes=[rk("ebl")])
        P.op("dve", lambda e: e.scalar_tensor_tensor(
            qtl[r][:], qts[b][:, h, cs], 0.125, E1[r][:], ALU.mult, ALU.mult),
            reads=[("qts", b, h), rk("E1")], writes=[rk("qtl")])
        P.op("dve", lambda e: e.tensor_tensor(ktl[r][:], kts[b][:, h, cs], E2[r][:], ALU.mult),
             reads=[("kts", b, h), rk("E2")], writes=[rk("ktl")])
        P.op("dve", lambda e: e.tensor_tensor(khat[r][:], kt_ps[r], E3[r][:], ALU.mult),
             reads=[rk("kt_ps"), rk("E3")], writes=[rk("khat")])

    def bk(i):
        g, c, h = items[i]
        b = g % 2
        t0 = g * GC
        r = i % NR
        cs = slice(c * 128, (c + 1) * 128)
        rk = lambda nm: (nm, r)
        P.op("pe", lambda e: e.matmul(sc_ps[r], ktl[r][:], qtl[r][:], start=True, stop=True),
             reads=[rk("ktl"), rk("qtl")], writes=[rk("sc_ps")])
        P.op("dve", lambda e: e.tensor_tensor(Asb[r][:], sc_ps[r], Us[:], ALU.mult),
             reads=[rk("sc_ps"), "Us"], writes=[rk("Asb")])
        P.op("pe", lambda e: e.matmul(o_ps[r], vtk[r][:], Asb[r][:], start=True, stop=False),
             reads=[rk("vtk"), rk("Asb")], writes=[rk("o_ps")])
        P.op("pe", lambda e: e.matmul(o_ps[r], state_b[h][:], qtl[r][:], start=False, stop=True),
             reads=[("stateb", h), rk("qtl")], writes=[rk("o_ps")])
        P.op("act", lambda e: e.copy(ogs[b][:, h, cs], o_ps[r]),
             reads=[rk("o_ps")], writes=[("ogs", b, h, c)])
        P.op("pe", lambda e: e.matmul(kv_ps[r], khat[r][:], vtk[r][:], start=True, stop=True),
             reads=[rk("khat"), rk("vtk")], writes=[rk("kv_ps")])
        P.op("dve", lambda e: e.scalar_tensor_tensor(
            state[h][:], state[h][:], ebl[r][:, 0:1], kv_ps[r], ALU.mult, ALU.add),
            reads=[("state", h), rk("ebl"), rk("kv_ps")], writes=[("state", h)])
        P.op("pool", lambda e: e.tensor_copy(state_b[h][:], state[h][:]),
             reads=[("state", h)], writes=[("stateb", h)])
        if c == NCH - 1 and h == 2:
            ch, c0 = divmod(t0, TOK)
            for hh in range(3):
                P.op("sp", lambda e, hh=hh: e.dma_start(
                    out=Bo2[ch * 384 + hh * 128:ch * 384 + (hh + 1) * 128, c0:c0 + GC], in_=ogs[b][:, hh, :]),
                    reads=[("ogs", b, hh, cc) for cc in range(NCH)], dma=True)

    for i in range(NI + 2):
        if i < NI:
            f1(i)
        if 0 <= i - 1 < NI:
            f2(i - 1)
        if 0 <= i - 2 < NI:
            bk(i - 2)


def kernel(**inp):
    inp = {k: np.asarray(v) for k, v in inp.items()}
    x = inp["x"]
    B, S, _ = x.shape
    CPB = NCORES // B
    TOK = S // CPB
    cores = list(range(NCORES))
    nc = _get("fused", build_fused, S)
    Lm, Um, U16, SU16, msk = _consts()
    shared = {
        "w_in": _f32(inp["ab_w_in"][0]), "gq": _f32(inp["sb_q_norm"][0].reshape(128, 1)),
        "gk": _f32(inp["sb_k_norm"][0].reshape(128, 1)), "g_og": _f32(inp["gla_o_norm"][0].reshape(128, 1)),
        "w_out": _f32(inp["ab_w_out"][0]), "w_cin": _f32(inp["conv_w_in"][0]),
        "cw": _f32(np.asarray(inp["conv_w"][0]).reshape(3, KC, 128).transpose(2, 1, 0)),
        "w_cout": _f32(inp["conv_w_out"][0]),
        "Lm": Lm, "Um": Um, "U16": U16, "SU16": SU16, "msk": msk, "ident": np.eye(128, dtype=np.float32),
    }
    wgas = []
    for j in range(4):
        w = np.zeros((32, 192), np.float32)
        w[0:16, :] = inp["gla_w_gate"][0][:, j * 192:(j + 1) * 192]
        w[16, :] = inp["gla_b_gate"][0][j * 192:(j + 1) * 192]
        wgas.append(w)
    for l in range(2):
        sf = "_%d" % l
        shared.update({
            "g1" + sf: _col(inp["ffn1_norm"][l]), "w_gu1" + sf: _f32(inp["ffn1_w_gu"][l]),
            "w_dn1" + sf: _f32(inp["ffn1_w_down"][l]), "gm" + sf: _col(inp["mix_norm"][l]),
            "g2" + sf: _col(inp["ffn2_norm"][l]), "w_gu2" + sf: _f32(inp["ffn2_w_gu"][l]),
            "w_dn2" + sf: _f32(inp["ffn2_w_down"][l]),
        })
    maps = []
    for c in cores:
        b, j = divmod(c, CPB)
        m = dict(shared)
        m["xT"] = _f32(x[b, j * TOK:(j + 1) * TOK, :].T)
        m["hsel"] = np.full((128, 1), 0.0 if j == 0 else 1.0, np.float32)
        m["wga"] = wgas[j]
        for l in range(2):
            for k, v in _xa_maps(inp, l, b).items():
                m[k + "_%d" % l] = v
        maps.append(m)
    res = run_bass_kernel_spmd(nc, maps, core_ids=cores).results
    out = np.empty((B, S, D), np.float32)
    for c in cores:
        b, j = divmod(c, CPB)
        out[b, j * TOK:(j + 1) * TOK, :] = res[c]["outT"].T
    return out
```
